# Optimizing a Trainium2 kernel written in Bass

```python
import jax, jax.numpy as jnp
from jax import lax
import numpy as np

D_MODEL = 1024
BATCH = 8
SEQ = 4096
DEPTH = 1

HEAD_DIM = 64
N_HEADS_A = 8
N_KV_A = 2
N_HEADS_B = 8
N_KV_B = 2
WIDTH_A = N_HEADS_A * HEAD_DIM
WIDTH_B = N_HEADS_B * HEAD_DIM
MIX_WIDTH = WIDTH_A + WIDTH_B
Q_BLOCK = 128
WINDOW = 128
GRID_W = 64
ROPE_THETA = 10000.0
EPS = 1e-6
D_FF = ((8 * D_MODEL + 3 * 256 - 1) // (3 * 256)) * 256
IN_COLS = (N_HEADS_A + 2 * N_KV_A + N_HEADS_B + 2 * N_KV_B) * HEAD_DIM
SPLITS = tuple(int(s) for s in np.cumsum([WIDTH_A, N_KV_A * HEAD_DIM, N_KV_A * HEAD_DIM,
                                           WIDTH_B, N_KV_B * HEAD_DIM])[:])

kernel_name = "hybrid_axial_global_windowed_sink_gqa_encoder"


def rms_norm(x, g):
    xf = x.astype(jnp.float32)
    y = xf * lax.rsqrt(jnp.mean(xf * xf, axis=-1, keepdims=True) + EPS)
    return (y * g.astype(jnp.float32)).astype(x.dtype)


def apply_rope(x, ang):
    cos = jnp.cos(ang)[None, :, None, :].astype(x.dtype)
    sin = jnp.sin(ang)[None, :, None, :].astype(x.dtype)
    x1, x2 = jnp.split(x, 2, axis=-1)
    return jnp.concatenate([x1 * cos - x2 * sin, x2 * cos + x1 * sin], axis=-1)


def global_attention(q, k, v):
    B, S, Hq, D = q.shape
    Hkv = k.shape[2]
    G = Hq // Hkv
    nb = S // Q_BLOCK
    qb = q.reshape(B, nb, Q_BLOCK, Hkv, G, D).transpose(1, 0, 2, 3, 4, 5)
    scale = D ** -0.5

    def one_block(qi):
        s = jnp.einsum('bqkgd,bskd->bkgqs', qi, k).astype(jnp.float32) * scale
        p = jax.nn.softmax(s, axis=-1).astype(v.dtype)
        return jnp.einsum('bkgqs,bskd->bqkgd', p, v)

    o = lax.map(one_block, qb)
    return o.transpose(1, 0, 2, 3, 4, 5).reshape(B, S, Hq * D)


def window_attention(q, k, v, sink):
    B, S, Hq, D = q.shape
    Hkv = k.shape[2]
    G = Hq // Hkv
    nb = S // Q_BLOCK
    C = 3 * Q_BLOCK
    qb = q.reshape(B, nb, Q_BLOCK, Hkv, G, D)
    pad = ((0, 0), (Q_BLOCK, Q_BLOCK), (0, 0), (0, 0))
    kp = jnp.pad(k, pad).reshape(B, nb + 2, Q_BLOCK, Hkv, D)
    vp = jnp.pad(v, pad).reshape(B, nb + 2, Q_BLOCK, Hkv, D)
    kb = jnp.concatenate([kp[:, :-2], kp[:, 1:-1], kp[:, 2:]], axis=2)
    vb = jnp.concatenate([vp[:, :-2], vp[:, 1:-1], vp[:, 2:]], axis=2)
    scale = D ** -0.5
    s = jnp.einsum('bnqkgd,bnckd->bnkgqc', qb, kb).astype(jnp.float32) * scale
    blk = jnp.arange(nb)[:, None, None]
    qpos = blk * Q_BLOCK + jnp.arange(Q_BLOCK)[None, :, None]
    kpos = (blk - 1) * Q_BLOCK + jnp.arange(C)[None, None, :]
    valid = (jnp.abs(kpos - qpos) <= WINDOW) & (kpos >= 0) & (kpos < S)
    s = jnp.where(valid[None, :, None, None], s, -jnp.inf)
    sink_l = sink.astype(jnp.float32).reshape(1, 1, Hkv, G, 1, 1)
    m = jnp.maximum(jnp.max(s, axis=-1, keepdims=True), sink_l)
    e = jnp.exp(s - m)
    den = jnp.sum(e, axis=-1, keepdims=True) + jnp.exp(sink_l - m)
    p = (e / den).astype(v.dtype)
    o = jnp.einsum('bnkgqc,bnckd->bnqkgd', p, vb)
    return o.reshape(B, S, Hq * D)


def setup_inputs(seed: int = 0) -> dict:
    key = jax.random.key(seed)
    ks = jax.random.split(key, 16)
    f32 = jnp.float32

    def gain(k, n):
        return 1.0 + 0.05 * jax.random.normal(k, (DEPTH, n), f32)

    def dense(k, fan_in, fan_out):
        return jax.random.normal(k, (DEPTH, fan_in, fan_out), f32) * fan_in ** -0.5

    return {
        "x": jax.random.normal(ks[0], (BATCH, SEQ, D_MODEL), f32),
        "norm_mix_pre": gain(ks[1], D_MODEL),
        "w_in": dense(ks[2], D_MODEL, IN_COLS),
        "q_norm_a": gain(ks[3], HEAD_DIM),
        "k_norm_a": gain(ks[4], HEAD_DIM),
        "sink_b": 0.5 * jax.random.normal(ks[5], (DEPTH, N_HEADS_B), f32),
        "group_norm_a": gain(ks[6], WIDTH_A),
        "group_norm_b": gain(ks[7], WIDTH_B),
        "w_out": dense(ks[8], MIX_WIDTH, D_MODEL),
        "norm_mix_post": gain(ks[9], D_MODEL),
        "norm_ffn_pre": gain(ks[10], D_MODEL),
        "w_gate": dense(ks[11], D_MODEL, D_FF),
        "w_up": dense(ks[12], D_MODEL, D_FF),
        "w_down": dense(ks[13], D_FF, D_MODEL),
        "norm_ffn_post": gain(ks[14], D_MODEL),
    }


def reference(x, norm_mix_pre, w_in, q_norm_a, k_norm_a, sink_b, group_norm_a, group_norm_b,
              w_out, norm_mix_post, norm_ffn_pre, w_gate, w_up, w_down, norm_ffn_post):
    B, S, _ = x.shape
    rows = S // GRID_W
    t = jnp.arange(S, dtype=jnp.float32)
    row = jnp.broadcast_to(jnp.arange(rows, dtype=jnp.float32)[:, None], (rows, GRID_W)).reshape(S)
    col = jnp.broadcast_to(jnp.arange(GRID_W, dtype=jnp.float32)[None, :], (rows, GRID_W)).reshape(S)
    ax_pairs = HEAD_DIM // 4
    freq_ax = ROPE_THETA ** (-jnp.arange(ax_pairs, dtype=jnp.float32) / ax_pairs)
    ang_axial = jnp.concatenate([row[:, None] * freq_ax[None, :], col[:, None] * freq_ax[None, :]], axis=-1)
    n_pairs = HEAD_DIM // 2
    freq_1d = ROPE_THETA ** (-jnp.arange(n_pairs, dtype=jnp.float32) / n_pairs)
    ang_1d = t[:, None] * freq_1d[None, :]

    for l in range(DEPTH):
        h = rms_norm(x, norm_mix_pre[l])
        proj = h @ w_in[l]
        qa, ka, va, qb, kb, vb = jnp.split(proj, SPLITS, axis=-1)

        qa = apply_rope(rms_norm(qa.reshape(B, S, N_HEADS_A, HEAD_DIM), q_norm_a[l]), ang_axial)
        ka = apply_rope(rms_norm(ka.reshape(B, S, N_KV_A, HEAD_DIM), k_norm_a[l]), ang_axial)
        va = va.reshape(B, S, N_KV_A, HEAD_DIM)
        out_a = global_attention(qa, ka, va)

        qb = apply_rope(qb.reshape(B, S, N_HEADS_B, HEAD_DIM), ang_1d)
        kb = apply_rope(kb.reshape(B, S, N_KV_B, HEAD_DIM), ang_1d)
        vb = vb.reshape(B, S, N_KV_B, HEAD_DIM)
        out_b = window_attention(qb, kb, vb, sink_b[l])

        mixed = jnp.concatenate([rms_norm(out_a, group_norm_a[l]),
                                 rms_norm(out_b, group_norm_b[l])], axis=-1) @ w_out[l]
        x = x + rms_norm(mixed, norm_mix_post[l])

        h = rms_norm(x, norm_ffn_pre[l])
        f = (jax.nn.silu(h @ w_gate[l]) * (h @ w_up[l])) @ w_down[l]
        x = x + rms_norm(f, norm_ffn_post[l])
    return x
```

```python
import numpy as np
from contextlib import ExitStack
import concourse.bass as bass
import concourse.mybir as mybir
from concourse.bass_utils import run_bass_kernel_spmd

F32 = mybir.dt.float32
BF16 = mybir.dt.bfloat16
AF = mybir.ActivationFunctionType
ALU = mybir.AluOpType
AX = mybir.AxisListType

D = 1024
DFF = 2816
NFC = DFF // 128
EPS = 1e-6
THETA = 10000.0

COMPUTE = ("pe", "act", "dve", "pool")
ALLQ = ("pe", "act", "dve", "pool", "sp")


class Res:
    __slots__ = ("name", "last_w", "rd_eng", "rd_dma", "sem", "dma_cnt", "excl", "last_real")

    def __init__(self, name, excl=False):
        self.name = name
        self.last_w = None
        self.rd_eng = {}
        self.rd_dma = []
        self.sem = None
        self.dma_cnt = 0
        self.excl = excl
        self.last_real = True


class Op:
    __slots__ = ("eng", "fn", "deps", "dma", "signal", "sig_val", "sem_res")

    def __init__(self, eng, fn, dma, sem_res):
        self.eng = eng
        self.fn = fn
        self.deps = []
        self.dma = dma
        self.signal = dma
        self.sig_val = 0
        self.sem_res = sem_res


class Prog:
    def __init__(self, nc, stack):
        self.nc = nc
        self.stack = stack
        self.ops = []
        self.dma_res = []

    def op(self, eng, fn, reads=(), writes=(), dma=False, sem_res=None):
        o = Op(eng, fn, dma, sem_res)
        deps = {}
        xreads = [r for r in reads if r.excl]
        reads = [r for r in reads if not r.excl]
        for r in xreads:
            p = r.last_w
            if p is not None and not (p.eng == eng and not r.last_real and not p.dma):
                deps[id(p)] = p
        for r in reads:
            if r.last_w is not None:
                deps[id(r.last_w)] = r.last_w
        for w in writes:
            if w.last_w is not None:
                deps[id(w.last_w)] = w.last_w
            for p in w.rd_eng.values():
                deps[id(p)] = p
            for p in w.rd_dma:
                deps[id(p)] = p
        for p in deps.values():
            if p.eng == "pe" and eng == "pe" and not p.dma and not dma:
                continue
            p.signal = True
            o.deps.append(p)
        for r in reads:
            if dma:
                r.rd_dma.append(o)
            else:
                r.rd_eng[eng] = o
        for w in writes:
            w.last_w = o
            w.last_real = True
            w.rd_eng = {}
            w.rd_dma = []
        for r in xreads:
            if r.last_w is not o:
                r.last_w = o
                r.last_real = False
        if dma:
            if sem_res.sem is None:
                sem_res.sem = self.stack.enter_context(self.nc.semaphore("d_" + sem_res.name))
                self.dma_res.append(sem_res)
            sem_res.dma_cnt += 1
            o.sig_val = 16 * sem_res.dma_cnt
        self.ops.append(o)
        return o

    def emit(self):
        nc = self.nc
        esem = {e: self.stack.enter_context(nc.semaphore("e_" + e)) for e in COMPUTE}
        cnt = {e: 0 for e in COMPUTE}
        if _os.environ.get("KALLSIG"):
            for o in self.ops:
                if o.eng in _os.environ["KALLSIG"].split(","):
                    o.signal = True
        for o in self.ops:
            if not o.dma and o.signal:
                cnt[o.eng] += 1
                o.sig_val = cnt[o.eng]
        per = {e: [] for e in ALLQ}
        for o in self.ops:
            per[o.eng].append(o)
        finals = [(r.sem, 16 * r.dma_cnt) for r in self.dma_res]

        def run(eng_name, eng):
            seen = {}
            for o in per[eng_name]:
                for p in o.deps:
                    if p.dma:
                        s, v = p.sem_res.sem, p.sig_val
                    else:
                        s, v = esem[p.eng], p.sig_val
                    k = id(s)
                    if seen.get(k, 0) < v:
                        eng.wait_ge(s, v)
                        seen[k] = v
                ins = o.fn(eng)
                if o.dma:
                    ins.then_inc(o.sem_res.sem, 16)
                elif o.signal:
                    ins.then_inc(esem[o.eng], 1)
            if eng_name == "sp":
                for s, v in finals:
                    eng.wait_ge(s, v)

        with nc.Block() as block:
            @block.sync
            def _(e):
                run("sp", e)

            @block.tensor
            def _(e):
                run("pe", e)

            @block.scalar
            def _(e):
                run("act", e)

            @block.vector
            def _(e):
                run("dve", e)

            @block.gpsimd
            def _(e):
                run("pool", e)
        return {e: len(per[e]) for e in ALLQ}, cnt


PG_KV = 0
PG_Q = 2
PG_O = 6
PG_G = 10
PG_U = 21
PG_D = 32
NPAGES = 44
NSLOT = 8


import os as _os


def DBG_BLOCKS(NB):
    v = _os.environ.get('KBLOCKS')
    return range(NB) if v is None else [int(t) for t in v.split(',')]


def build_program(S):
    NT = S // 128
    NB = S // 512
    nc = bass.Bass("TRN2", target_bir_lowering=False)

    def din(name, shape):
        return nc.dram_tensor(name, list(shape), F32, kind="ExternalInput").ap()

    x_d = din("x", [S, D])
    wkv_d = din("w_kv", [D, 512])
    wq_d = din("w_q", [D, 1024])
    wo_d = din("w_o", [D, 1024])
    wg_d = din("w_g", [D, DFF])
    wu_d = din("w_u", [D, DFF])
    wd_d = din("w_d", [DFF, D])
    gains_d = din("gains", [4, D])
    qk_d = din("qk_gain", [4, 64])
    sink_d = din("sink", [1, 8])
    gcol_d = din("gcol", [128, 8])
    rope_d = din("rope", [S, 256])
    ident_d = din("ident", [128, 128])
    mask_d = din("mask", [128, 384])
    out_d = nc.dram_tensor("out", [S, D], F32, kind="ExternalOutput").ap()
    dbg_d = nc.dram_tensor("dbg", [S, D], F32, kind="ExternalOutput").ap() if _os.environ.get("KDBG") else None
    scr = nc.dram_tensor("wscr", [NPAGES, 128, 2048], BF16).ap()
    dbga_d = nc.dram_tensor("dbga", [128, 8, 512], F32, kind="ExternalOutput").ap() if _os.environ.get("KDBGA") else None

    with ExitStack() as st:
        def sb(name, shape, dt):
            return st.enter_context(nc.sbuf_tensor("s_" + name, list(shape), dt))

        pages = sb("pages", [128, NSLOT, 2048], BF16)
        kT = sb("kT", [128, 2, S], BF16)
        V = sb("V", [128, NT, 6, 64], BF16)
        gains = sb("gains", [128, 4, D], F32)
        qkg = sb("qkg", [128, 4, 64], F32)
        es = sb("es", [128, 8], F32)
        gcol = sb("gcol", [128, 8], F32)
        identb = sb("identb", [128, 128], BF16)
        maskb = sb("maskb", [128, 384], BF16)
        ones_c = sb("ones_c", [128, 8], BF16)
        epsc = sb("epsc", [128, 8], F32)
        xblk = sb("xblk", [128, 4, D], F32)
        ropet = sb("ropet", [128, 4, 256], F32)
        tabs = sb("tabs", [128, 2, 2, 64], F32)
        xn = sb("xn", [128, 2, D], BF16)
        hT = sb("hT", [128, 8, 512], BF16)
        qtok = sb("qtok", [128, D], F32)
        tq = sb("tq", [128, 512], F32)
        sw = sb("sw", [128, 512], F32)
        qrot = sb("qrot", [128, 2, D], BF16)
        qT = sb("qT", [128, 8, 512], BF16)
        PT = sb("PT", [128, 3, 1024], BF16)
        PTW = sb("PTW", [128, 1, 768], BF16)
        junk = PT[:, 0, :]
        attnT = sb("attnT", [128, 8, 512], BF16)
        ro = sb("ro", [128, 5, 512], F32)
        rec = ro[:, 0:2, :]
        onrm = ro[:, 2:5, :]
        sqb = sb("sqb", [128, 8, 512], BF16)
        mixed = sb("mixed", [128, D], F32)
        aT = sb("aT", [128, NFC, 512], BF16)
        sg = sb("sg", [128, 2, 512], F32)
        stat = sb("stat", [128, 64], F32)

        pball = st.enter_context(nc.psum_tensor("pball", [128, 8, 512], F32))
        pb = [pball[:, i, :] for i in range(8)]
        pp = [st.enter_context(nc.psum_tensor("pp%d" % i, [128, 1024], F32)) for i in range(0)]

        P = Prog(nc, st)
        R = {}

        def res(name, excl=False):
            if name not in R:
                R[name] = Res(name, excl)
            return R[name]

        BK = [res("bank%d" % i, excl=True) for i in range(8)]
        scr_res = [res("scr%d" % i) for i in range(NPAGES)]

        def cast_page(pg, src_ap):
            P.op("pool", lambda e: e.dma_start(out=scr[pg].rearrange("p (a f) -> p a f", a=src_ap.shape[1]),
                                               in_=src_ap),
                 writes=[scr_res[pg]], dma=True, sem_res=scr_res[pg])

        wkv_v = wkv_d.rearrange("(kc p) f -> p kc f", p=128)
        wq_v = wq_d.rearrange("(kc p) f -> p kc f", p=128)
        wo_v = wo_d.rearrange("(kc p) f -> p kc f", p=128)
        wg_v = wg_d.rearrange("(kc p) f -> p kc f", p=128)
        wu_v = wu_d.rearrange("(kc p) f -> p kc f", p=128)
        wd_v = wd_d.rearrange("(fc p) d -> p fc d", p=128)
        for k in range(2):
            cast_page(PG_KV + k, wkv_v[:, :, k * 256:(k + 1) * 256])
        cres = res("consts")
        P.op("sp", lambda e: e.dma_start(out=gains[:].rearrange("p a d -> p (a d)"),
                                         in_=gains_d.rearrange("a d -> (a d)").partition_broadcast(128)),
             writes=[res("gains")], dma=True, sem_res=res("gains"))
        P.op("sp", lambda e: e.dma_start(out=qkg[:].rearrange("p a d -> p (a d)"),
                                         in_=qk_d.rearrange("a d -> (a d)").partition_broadcast(128)),
             writes=[res("qkg")], dma=True, sem_res=res("qkg"))
        P.op("sp", lambda e: e.dma_start(out=es[:], in_=sink_d.rearrange("a d -> (a d)").partition_broadcast(128)),
             writes=[res("es")], dma=True, sem_res=res("es"))
        P.op("sp", lambda e: e.dma_start(out=gcol[:], in_=gcol_d), writes=[res("gcol")], dma=True, sem_res=res("gcol"))
        P.op("pool", lambda e: e.dma_start(out=identb[:], in_=ident_d), writes=[res("identb")], dma=True,
             sem_res=res("identb"))
        P.op("pool", lambda e: e.dma_start(out=maskb[:], in_=mask_d), writes=[res("maskb")], dma=True,
             sem_res=res("maskb"))
        for k in range(4):
            cast_page(PG_Q + k, wq_v[:, :, k * 256:(k + 1) * 256])
        for k in range(4):
            cast_page(PG_O + k, wo_v[:, :, k * 256:(k + 1) * 256])
        for k in range(11):
            cast_page(PG_G + k, wg_v[:, :, k * 256:(k + 1) * 256])
            cast_page(PG_U + k, wu_v[:, :, k * 256:(k + 1) * 256])
        for cb in range(2):
            for grp in range(6):
                nfc = min(4, NFC - 4 * grp)
                pg = PG_D + cb * 6 + grp
                P.op("pool", lambda e, pg=pg, grp=grp, nfc=nfc, cb=cb: e.dma_start(
                    out=scr[pg].rearrange("p (a f) -> p a f", a=4)[:, 0:nfc, :],
                    in_=wd_v[:, 4 * grp:4 * grp + nfc, cb * 512:(cb + 1) * 512]),
                    writes=[scr_res[pg]], dma=True, sem_res=scr_res[pg])
        P.op("act", lambda e: e.activation(out=es[:], in_=es[:], func=AF.Exp), reads=[res("es")], writes=[res("es")])
        P.op("pool", lambda e: e.memset(V[:, :, 1, :], 1.0), writes=[res("Vones")])
        P.op("pool", lambda e: e.memset(V[:, :, 4, :], 1.0), writes=[res("Vones")])
        P.op("pool", lambda e: e.memset(ones_c[:], 1.0), writes=[res("ones_c")])
        P.op("pool", lambda e: e.memset(epsc[:], EPS), writes=[res("epsc")])

        sched = [PG_KV, PG_KV + 1]
        for j in DBG_BLOCKS(NB):
            sched += [PG_Q + k for k in range(4)]
            sched += [PG_O + k for k in range(4)]
            for k in range(11):
                sched += [PG_G + k, PG_U + k]
            sched += [PG_D + k for k in range(12)]
        slot_res = [res("slot%d" % s) for s in range(NSLOT)]
        pstate = {"next": 0, "fifo": []}

        def page_load(slot):
            i = pstate["next"]
            if i >= len(sched):
                return
            pstate["next"] = i + 1
            pg = sched[i]
            w = 1024 if pg in (PG_D + 5, PG_D + 11) else 2048
            P.op("sp", lambda e, slot=slot, pg=pg, w=w: e.dma_start(out=pages[:, slot, 0:w], in_=scr[pg][:, 0:w]),
                 reads=[scr_res[pg]], writes=[slot_res[slot]], dma=True, sem_res=slot_res[slot])
            pstate["fifo"].append((slot, pg))

        def page_acquire(expect):
            slot, pg = pstate["fifo"].pop(0)
            assert pg == expect, (pg, expect)
            return slot

        for s in range(NSLOT):
            page_load(s)

        def rstd_from_ss(ss_ap, ss_res, n, k=1):
            P.op("act", lambda e: e.activation(out=ss_ap, in_=ss_ap, func=AF.Ln, scale=1.0 / n, bias=epsc[:, 0:1]),
                 reads=[ss_res, res("epsc")], writes=[ss_res])
            P.op("act", lambda e: e.activation(out=ss_ap, in_=ss_ap, func=AF.Exp, scale=-0.5),
                 reads=[ss_res], writes=[ss_res])

        def transposes_to(dst_fn, src_ap_fn, src_res, n, banks, dst_res, tag):
            for c in range(n):
                b = banks[c // 4]
                P.op("pe", lambda e, c=c, b=b: e.matmul(pb[b][:, (c % 4) * 128:(c % 4 + 1) * 128],
                                                         lhsT=src_ap_fn(c), rhs=identb[:], start=True, stop=True),
                     reads=[src_res, res("identb")], writes=[BK[b]])
            if n == 8 and banks[1] == banks[0] + 1:
                b0 = banks[0]
                P.op("act", lambda e, b0=b0: e.activation(
                    out=dst_fn(0, 8), in_=pball[:, b0:b0 + 2, :].rearrange("p a (c t) -> p (a c) t", t=128),
                    func=AF.Copy), reads=[BK[b0], BK[b0 + 1]], writes=[dst_res])
                return
            for h in range((n + 3) // 4):
                b = banks[h]
                lo, hi = 4 * h, min(4 * h + 4, n)
                P.op("act", lambda e, b=b, lo=lo, hi=hi: e.activation(
                    out=dst_fn(lo, hi), in_=pb[b][:, 0:(hi - lo) * 128].rearrange("p (c t) -> p c t", t=128),
                    func=AF.Copy), reads=[BK[b]], writes=[dst_res])

        def norm_tile(x_ap, x_res, gain_idx, xn_slot, ss_col, tag):
            ssr = res("ss_col%d" % ss_col)
            ss_ap = stat[:, ss_col:ss_col + 1]
            P.op("act", lambda e: e.activation(out=junk[:], in_=x_ap, func=AF.Square, accum_out=ss_ap),
                 reads=[x_res], writes=[res("PT0"), ssr])
            rstd_from_ss(ss_ap, ssr, D)
            P.op("dve", lambda e: e.scalar_tensor_tensor(out=xn[:, xn_slot, :], in0=x_ap, scalar=ss_ap,
                                                         in1=gains[:, gain_idx, :], op0=ALU.mult, op1=ALU.mult),
                 reads=[x_res, ssr, res("gains")], writes=[res("xn%d" % xn_slot)])

        def rope(dst_ap, src_ap, H, c_ap, s_ap, reads, dst_res, eng2="pool"):
            s3 = src_ap.rearrange("p (h d) -> p h d", d=64)
            t3 = tq[:, 0:H * 64].rearrange("p (h d) -> p h d", d=64)
            w3 = sw[:, 0:H * 64].rearrange("p (h d) -> p h d", d=64)
            d3 = dst_ap.rearrange("p (h d) -> p h d", d=64)
            cb_ = c_ap.unsqueeze(1).to_broadcast([128, H, 64])
            P.op("dve", lambda e: e.tensor_tensor(out=t3, in0=s3, in1=cb_, op=ALU.mult),
                 reads=reads, writes=[res("tq")])
            P.op(eng2, lambda e: e.tensor_tensor(out=w3[:, :, 0:32], in0=s3[:, :, 32:64],
                                                 in1=s_ap[:, 0:32].unsqueeze(1).to_broadcast([128, H, 32]), op=ALU.mult),
                 reads=reads, writes=[res("sw")])
            P.op(eng2, lambda e: e.tensor_tensor(out=w3[:, :, 32:64], in0=s3[:, :, 0:32],
                                                 in1=s_ap[:, 32:64].unsqueeze(1).to_broadcast([128, H, 32]), op=ALU.mult),
                 reads=reads, writes=[res("sw")])
            P.op("dve", lambda e: e.tensor_tensor(out=d3, in0=t3, in1=w3, op=ALU.add),
                 reads=[res("tq"), res("sw")], writes=[dst_res])

        def headnorm(src_ap, src_res, H, ss_col, tag):
            ssr = res("hss")
            ss_ap = stat[:, ss_col:ss_col + H]
            s3 = src_ap.rearrange("p (h d) -> p h d", d=64)
            P.op("act", lambda e: e.activation(out=sw[:, 0:H * 64], in_=src_ap, func=AF.Square),
                 reads=[src_res], writes=[res("sw")])
            P.op("dve", lambda e: e.tensor_reduce(out=ss_ap, in_=sw[:, 0:H * 64].rearrange("p (h d) -> p h d", d=64),
                                                  axis=AX.X, op=ALU.add), reads=[res("sw")], writes=[ssr])
            rstd_from_ss(ss_ap, ssr, 64)
            P.op("dve", lambda e: e.tensor_tensor(out=s3, in0=s3, in1=ss_ap.unsqueeze(2).to_broadcast([128, H, 64]),
                                                  op=ALU.mult), reads=[src_res, ssr], writes=[src_res])

        def fold_tables(slot, rope_ap, gidx):
            tr = res("tabs%d" % slot)
            P.op("pool", lambda e: e.tensor_tensor(out=tabs[:, slot, 0, :], in0=rope_ap[:, 0:64], in1=qkg[:, gidx, :],
                                                   op=ALU.mult), reads=[res("ropet"), res("qkg")], writes=[tr])
            P.op("pool", lambda e: e.tensor_tensor(out=tabs[:, slot, 1, :], in0=rope_ap[:, 64:128],
                                                   in1=qkg[:, gidx + 1, :], op=ALU.mult),
                 reads=[res("ropet"), res("qkg")], writes=[tr])
            return tr

        sgflat = sg[:].rearrange("p a c -> p (a c)")
        P1Q = [(qtok[:, k * 256:(k + 1) * 256], [res("qtok_p1_%d" % k)]) for k in range(4)]
        QS = [(qtok, [res("qtok")] + [res("qtok_p1_%d" % k) for k in range(4)]), (sgflat, [res("sg0"), res("sg1")])]
        NCOL = [0, 3]
        HCOL = [8, 24]
        hsq = mixed[:, 0:512]

        def pipeline(tiles, stages):
            n, ns = len(tiles), len(stages)
            for step in range(n + ns - 1):
                for k in reversed(range(ns)):
                    t = step - k
                    if 0 <= t < n:
                        stages[k](tiles[t])

        def st_load_x(i):
            tt = i % 4
            xr = res("xblk%d" % tt)
            P.op("sp", lambda e: e.dma_start(out=xblk[:, tt, :], in_=x_d[i * 128:(i + 1) * 128, :]),
                 writes=[xr], dma=True, sem_res=xr)

        def st_load_rope(i):
            tt = i % 4
            rr_ = res("ropet%d" % tt)
            P.op("sp", lambda e: e.dma_start(out=ropet[:, tt, :], in_=rope_d[i * 128:(i + 1) * 128, :]),
                 writes=[rr_], dma=True, sem_res=rr_)

        def st_ss(i):
            tt = i % 4
            xr = res("xblk%d" % tt)
            col = NCOL[i % 2]
            ssr = res("ss_col%d" % col)
            ss_ap = stat[:, col:col + 1]
            P.op("act", lambda e: e.activation(out=junk[:], in_=xblk[:, tt, :], func=AF.Square, accum_out=ss_ap),
                 reads=[xr], writes=[res("PT0"), ssr])
            rstd_from_ss(ss_ap, ssr, D)

        def st_xn(i):
            tt = i % 4
            xr = res("xblk%d" % tt)
            col = NCOL[i % 2]
            sl = i % 2
            P.op("dve", lambda e: e.scalar_tensor_tensor(out=xn[:, sl, :], in0=xblk[:, tt, :], scalar=stat[:, col:col + 1],
                                                         in1=gains[:, 0, :], op0=ALU.mult, op1=ALU.mult),
                 reads=[xr, res("ss_col%d" % col), res("gains")], writes=[res("xn%d" % sl)])

        def st_T(i, tb):
            tt = i % 4
            sl = i % 2
            transposes_to(lambda lo, hi: hT[:, lo:hi, tt * 128:(tt + 1) * 128],
                          lambda c: xn[:, sl, c * 128:(c + 1) * 128], res("xn%d" % sl), 8, tb, res("hT%d" % tt), "x")

        def st_xn_T(i, tb):
            tt = i % 4
            xr = res("xblk%d" % tt)
            col = NCOL[i % 2]
            sl = i % 2
            P.op("dve", lambda e: e.scalar_tensor_tensor(out=xn[:, sl, :], in0=xblk[:, tt, :], scalar=stat[:, col:col + 1],
                                                         in1=gains[:, 0, :], op0=ALU.mult, op1=ALU.mult),
                 reads=[xr, res("ss_col%d" % col), res("gains")], writes=[res("xn%d" % sl)])
            transposes_to(lambda lo, hi: hT[:, lo:hi, tt * 128:(tt + 1) * 128],
                          lambda c: xn[:, sl, c * 128:(c + 1) * 128], res("xn%d" % sl), 8, tb, res("hT%d" % tt), "x")

        def st_headnorm(i, H, gidx, part=None, slots=None):
            qs, qres = (slots[i % len(slots)] if slots is not None else QS[i % 2])
            tt = i % 4
            hc = HCOL[i % 2]
            ssr = res("hss%d" % (i % 2))
            ss_ap = stat[:, hc:hc + H]
            src = qs[:, 0:H * 64]
            s3 = src.rearrange("p (h d) -> p h d", d=64)
            if part in (None, "a"):
                P.op("act", lambda e: e.activation(out=hsq[:, 0:H * 64], in_=src, func=AF.Square),
                     reads=qres, writes=[res("mixed")])
                P.op("dve", lambda e: e.tensor_reduce(out=ss_ap, in_=hsq[:, 0:H * 64].rearrange("p (h d) -> p h d", d=64),
                                                      axis=AX.X, op=ALU.add), reads=[res("mixed")], writes=[ssr])
            if part == "a":
                return
            rstd_from_ss(ss_ap, ssr, 64)
            P.op("dve", lambda e: e.tensor_tensor(out=s3, in0=s3, in1=ss_ap.unsqueeze(2).to_broadcast([128, H, 64]),
                                                  op=ALU.mult), reads=qres + [ssr], writes=qres)
            tr = res("tabs%d" % (i % 2))
            rr_ = res("ropet%d" % tt)
            P.op("pool", lambda e: e.tensor_tensor(out=tabs[:, i % 2, 0, :], in0=ropet[:, tt, 0:64], in1=qkg[:, gidx, :],
                                                   op=ALU.mult), reads=[rr_, res("qkg")], writes=[tr])
            P.op("pool", lambda e: e.tensor_tensor(out=tabs[:, i % 2, 1, :], in0=ropet[:, tt, 64:128],
                                                   in1=qkg[:, gidx + 1, :], op=ALU.mult),
                 reads=[rr_, res("qkg")], writes=[tr])

        def st_rope(i, H, slots=None):
            qs, qres = (slots[i % len(slots)] if slots is not None else QS[i % 2])
            tt = i % 4
            sl = i % 2
            qr = res("qrot%d" % sl)
            rope(qrot[:, sl, 0:H * 64], qs[:, 0:H * 64], H, tabs[:, sl, 0, :], tabs[:, sl, 1, :],
                 qres + [res("tabs%d" % sl)], qr)
            rope(qrot[:, sl, H * 64:2 * H * 64], qs[:, H * 64:2 * H * 64], H, ropet[:, tt, 128:192], ropet[:, tt, 192:256],
                 qres + [res("ropet%d" % tt)], qr)

        s_kv0 = page_acquire(PG_KV)
        s_kv1 = page_acquire(PG_KV + 1)

        def p1_proj(i):
            tt = i % 4
            st_load_rope(i)
            kb = 4 + (i % 2)
            qs, qres = P1Q[i % 4]
            for half, sl in ((0, s_kv0), (1, s_kv1)):
                for kc in range(8):
                    P.op("pe", lambda e, kc=kc, half=half, sl=sl: e.matmul(
                        pb[kb][:, half * 256:(half + 1) * 256], lhsT=hT[:, kc, tt * 128:(tt + 1) * 128],
                        rhs=pages[:, sl, kc * 256:(kc + 1) * 256], start=(kc == 0), stop=(kc == 7)),
                        reads=[res("hT%d" % tt), slot_res[sl]], writes=[BK[kb]])
            P.op("act", lambda e: e.activation(out=qs[:, 0:256], in_=pb[kb][:, 0:256], func=AF.Copy),
                 reads=[BK[kb]], writes=qres)
            P.op("dve", lambda e: e.tensor_copy(out=V[:, i, 0, :], in_=pb[kb][:, 256:320]),
                 reads=[BK[kb]], writes=[res("V")])
            P.op("dve", lambda e: e.tensor_copy(out=V[:, i, 5, :], in_=pb[kb][:, 320:384]),
                 reads=[BK[kb]], writes=[res("V")])
            P.op("dve", lambda e: e.tensor_copy(
                out=V[:, i, 2:4, :], in_=pb[kb][:, 384:512].rearrange("p (a d) -> p a d", d=64)),
                reads=[BK[kb]], writes=[res("V")])

        def p1_kT(i):
            sl = i % 2
            kbk = 6 + sl
            for c in range(2):
                P.op("pe", lambda e, c=c: e.matmul(pb[kbk][:, c * 128:(c + 1) * 128],
                                                    lhsT=qrot[:, sl, c * 128:(c + 1) * 128], rhs=identb[:],
                                                    start=True, stop=True),
                     reads=[res("qrot%d" % sl), res("identb")], writes=[BK[kbk]])
            P.op("act", lambda e: e.activation(
                out=kT[:, :, i * 128:(i + 1) * 128], in_=pb[kbk][:, 0:256].rearrange("p (c t) -> p c t", t=128),
                func=AF.Copy), reads=[BK[kbk]], writes=[res("kT")])

        pipeline(list(range(NT)), [
            st_load_x, st_ss, st_xn, lambda i: st_T(i, [(0, 1), (2, 3)][i % 2]), p1_proj,
            lambda i: st_headnorm(i, 2, 2, "a", P1Q), lambda i: st_headnorm(i, 2, 2, "b", P1Q),
            lambda i: st_rope(i, 2, P1Q), p1_kT])
        page_load(s_kv0)
        page_load(s_kv1)

        preloaded = {"x": False}
        blocks_ = list(DBG_BLOCKS(NB))
        for bi_, j in enumerate(blocks_):
            t0 = 4 * j
            sq_slots = [page_acquire(PG_Q + k) for k in range(4)]
            if not preloaded["x"]:
                for tt in range(4):
                    st_load_x(t0 + tt)
                    st_load_rope(t0 + tt)
            preloaded["x"] = False

            def ab_proj(i, sq_slots=sq_slots):
                tt = i % 4
                qb_ = (2, 3) if tt % 2 == 0 else (4, 5)
                qs, qres = QS[i % 2]
                for k in range(4):
                    b = qb_[k // 2]
                    for kc in range(8):
                        P.op("pe", lambda e, k=k, kc=kc, b=b: e.matmul(
                            pb[b][:, (k % 2) * 256:(k % 2 + 1) * 256], lhsT=hT[:, kc, tt * 128:(tt + 1) * 128],
                            rhs=pages[:, sq_slots[k], kc * 256:(kc + 1) * 256], start=(kc == 0), stop=(kc == 7)),
                            reads=[res("hT%d" % tt), slot_res[sq_slots[k]]], writes=[BK[b]])
                P.op("act", lambda e: e.activation(out=qs[:, 0:1024].rearrange("p (a c) -> p a c", a=2),
                                                   in_=pball[:, qb_[0]:qb_[0] + 2, :], func=AF.Copy),
                     reads=[BK[qb_[0]], BK[qb_[1]]], writes=qres)

            def ab_qT(i):
                tt = i % 4
                sl = i % 2
                transposes_to(lambda lo, hi: qT[:, lo:hi, tt * 128:(tt + 1) * 128],
                              lambda c: qrot[:, sl, c * 128:(c + 1) * 128], res("qrot%d" % sl), 8, (6, 7), res("qT"), "q")

            pipeline([t0 + tt for tt in range(4)], [
                st_ss, st_xn, lambda i: st_T(i, (0, 1)), ab_proj,
                lambda i: st_headnorm(i, 8, 0), lambda i: st_rope(i, 8), ab_qT])
            for s_ in sq_slots:
                page_load(s_)

            d_items = [(g, half, nh) for g in range(4) for half in range(2) for nh in range(2)]
            items = []
            for g in range(4):
                for kt in range(NT):
                    items.append(("C", g, kt))
                    if len([1 for it in items if it[0] == "C"]) % 8 == 0 and d_items:
                        items.append(("D",) + d_items.pop(0))
            while d_items:
                items.append(("D",) + d_items.pop(0))
            NI = len(items)
            SBK = ((0, 1), (2, 3))
            pending_stats = []
            cur_iter = {"u": 0}
            PTW4 = PTW
            dcount = {"n": 0}
            dslot = {}

            def xbank(g, half):
                return 4 + (2 * g + half) % 3

            def wvalid(n):
                qi = t0 + n
                return [m for m in (qi - 1, qi, qi + 1) if 0 <= m < NT]

            def flush_stats(b1, now=None, n_max=8):
                colbase = 256
                while pending_stats and n_max > 0 and (now is None or pending_stats[0][0] <= now):
                    pending_stats.pop(0)[1](b1, colbase)
                    colbase += 4
                    n_max -= 1

            def emit_S(u):
                it = items[u]
                b0, b1 = SBK[u % 2]
                if it[0] == "C":
                    _, g, kt = it
                    for half in range(2):
                        b = SBK[u % 2][half]
                        P.op("pe", lambda e, g=g, half=half, kt=kt, b=b: e.matmul(
                            pb[b], lhsT=kT[half * 64:(half + 1) * 64, 0, kt * 128:(kt + 1) * 128],
                            rhs=qT[half * 64:(half + 1) * 64, g, :], start=True, stop=True),
                            reads=[res("kT"), res("qT")], writes=[BK[b]])
                else:
                    _, g, half, nh = it
                    ps2 = pball[:, b0:b0 + 2, :].rearrange("p a c -> p (a c)")
                    for nl in range(2):
                        n = 2 * nh + nl
                        qi = t0 + n
                        for m in wvalid(n):
                            mi = m - (qi - 1)
                            col = nl * 384 + mi * 128
                            P.op("pe", lambda e, m=m, col=col, g=g, half=half, n=n, ps2=ps2, mi=mi: e.matmul(
                                ps2[:, col:col + 128], lhsT=kT[half * 64:(half + 1) * 64, 1, m * 128:(m + 1) * 128],
                                rhs=qT[half * 64:(half + 1) * 64, 4 + g, n * 128:(n + 1) * 128], start=True, stop=(mi == 1)),
                                reads=[res("kT"), res("qT")], writes=[BK[b0 + col // 512]])
                            if mi != 1:
                                P.op("pe", lambda e, col=col, ps2=ps2, mi=mi: e.matmul(
                                    ps2[:, col:col + 128], lhsT=identb[:], rhs=maskb[:, mi * 128:(mi + 1) * 128],
                                    start=False, stop=True),
                                    reads=[res("identb"), res("maskb")], writes=[BK[b0 + col // 512]])

            def emit_E(u):
                it = items[u]
                b0, b1 = SBK[u % 2]
                if it[0] == "C":
                    pt = u % 3
                    P.op("act", lambda e, b0=b0, pt=pt: e.activation(
                        out=PT[:, pt, :].rearrange("p (a c) -> p a c", a=2), in_=pball[:, b0:b0 + 2, :], func=AF.Exp, scale=0.125),
                        reads=[BK[b0], BK[b1]], writes=[res("PT%d" % pt)])
                else:
                    pw = 0
                    dcount["n"] += 1
                    dslot[u] = pw
                    pr = res("PTW%d" % pw)
                    ps2 = pball[:, b0:b0 + 2, :].rearrange("p a c -> p (a c)")
                    P.op("act", lambda e, pw=pw, ps2=ps2: e.activation(out=PTW4[:, pw, :], in_=ps2[:, 0:768], func=AF.Exp, scale=0.125),
                         reads=[BK[b0], BK[b1]], writes=[pr])

            def chunk_finish(c, sl, first):
                qsl = c
                gate = 0
                onr, sqr = res("onrm%d" % sl), res("sqb%d" % qsl)
                P.op("pool", lambda e: e.tensor_tensor(out=sqb[:, qsl, :], in0=onrm[:, sl, :], in1=onrm[:, sl, :],
                                                       op=ALU.mult), reads=[onr], writes=[sqr])
                P.op("dve", lambda e: e.tensor_scalar(out=attnT[:, c, :], in0=onrm[:, sl, :], scalar1=gcol[:, c:c + 1],
                                                      scalar2=None, op0=ALU.mult),
                     reads=[onr, res("gcol")], writes=[res("attnT%d" % c)])
                gbase = 16 + 4 * (c // 4)

                def emit_stats(b1, colbase):
                    for tt in range(4):
                        P.op("pe", lambda e, tt=tt: e.matmul(
                            pb[b1][:, colbase + tt:colbase + tt + 1], lhsT=sqb[:, qsl, tt * 128:(tt + 1) * 128], rhs=ones_c[:, 0:1],
                            start=True, stop=True, skip_group_check=True), reads=[sqr, res("ones_c")], writes=[BK[b1]])
                    if first:
                        P.op("dve", lambda e: e.tensor_copy(out=stat[:, gbase:gbase + 4], in_=pb[b1][:, colbase:colbase + 4]),
                             reads=[BK[b1]], writes=[res("gss")])
                    else:
                        P.op("dve", lambda e: e.tensor_tensor(out=stat[:, gbase:gbase + 4], in0=stat[:, gbase:gbase + 4],
                                                              in1=pb[b1][:, colbase:colbase + 4], op=ALU.add),
                             reads=[BK[b1], res("gss")], writes=[res("gss")])
                pending_stats.append((cur_iter["u"] + gate, emit_stats))

            def epi_half(c, bank, o_lo, den_lo, sink_head, urgent=False):
                rs = c % 2
                sl = c % 2 if c < 4 else 2
                rr, onr = res("rec%d" % rs), res("onrm%d" % sl)
                if urgent and not _os.environ.get('KNOURGENT'):
                    P.op("act", lambda e: e.activation(out=rec[:, rs, :], in_=pb[bank], func=AF.Copy), reads=[BK[bank]], writes=[rr])
                else:
                    P.op("dve", lambda e: e.tensor_copy(out=rec[:, rs, :], in_=pb[bank]), reads=[BK[bank]], writes=[rr])
                if sink_head is None:
                    P.op("dve", lambda e: e.reciprocal(out=onrm[o_lo:o_lo + 64, sl, :], in_=rec[den_lo:den_lo + 64, rs, :]),
                         reads=[rr], writes=[onr])
                else:
                    P.op("dve", lambda e: e.tensor_scalar(out=onrm[o_lo:o_lo + 64, sl, :], in0=rec[den_lo:den_lo + 64, rs, :],
                                                          scalar1=es[o_lo:o_lo + 64, sink_head:sink_head + 1], scalar2=None,
                                                          op0=ALU.add), reads=[rr, res("es")], writes=[onr])
                    P.op("dve", lambda e: e.reciprocal(out=onrm[o_lo:o_lo + 64, sl, :], in_=onrm[o_lo:o_lo + 64, sl, :]),
                         reads=[onr], writes=[onr])
                P.op("dve", lambda e: e.tensor_tensor(out=onrm[o_lo:o_lo + 64, sl, :], in0=rec[o_lo:o_lo + 64, rs, :],
                                                      in1=onrm[o_lo:o_lo + 64, sl, :], op=ALU.mult), reads=[rr, onr], writes=[onr])

            def emit_PV(u):
                it = items[u]
                if it[0] == "C":
                    _, g, kt = it
                    pt = u % 3
                    ptr = res("PT%d" % pt)
                    for half in range(2):
                        xb_ = xbank(g, half)
                        vb = (0, 2) if half == 0 else (4, 6)
                        P.op("pe", lambda e, kt=kt, pt=pt, xb_=xb_, vb=vb, half=half: e.matmul(
                            pb[xb_], lhsT=V[:, kt, vb[0]:vb[1], :].rearrange("p a d -> p (a d)"),
                            rhs=PT[:, pt, half * 512:(half + 1) * 512],
                            start=(kt == 0), stop=(kt == NT - 1)), reads=[res("V"), res("Vones"), ptr], writes=[BK[xb_]])
                    if kt == NT - 1:
                        epi_half(g, xbank(g, 0), 0, 64, None, urgent=True)
                        epi_half(g, xbank(g, 1), 64, 0, None)
                        chunk_finish(g, g % 2, g == 0)
                else:
                    _, g, half, nh = it
                    pw = dslot[u]
                    pr = res("PTW%d" % pw)
                    vb = (1, 3) if half == 0 else (3, 5)
                    for nl in range(2):
                        n = 2 * nh + nl
                        qi = t0 + n
                        ms = wvalid(n)
                        for m in ms:
                            col = nl * 384 + (m - (qi - 1)) * 128
                            P.op("pe", lambda e, m=m, col=col, pw=pw, vb=vb, n=n, f=(m == ms[0]), l=(m == ms[-1]): e.matmul(
                                pb[7][:, n * 128:(n + 1) * 128], lhsT=V[:, m, vb[0]:vb[1], :].rearrange("p a d -> p (a d)"),
                                rhs=PTW4[:, pw, col:col + 128], start=f, stop=l, skip_group_check=True),
                                reads=[res("V"), res("Vones"), pr], writes=[BK[7]])
                    if nh == 1:
                        c = 4 + g
                        if half == 0:
                            epi_half(c, 7, 64, 0, g)
                        else:
                            epi_half(c, 7, 0, 64, 4 + g)
                            chunk_finish(c, 2, g == 0)

            emit_S(0)
            if NI > 1:
                emit_S(1)
            late = []
            for u in range(NI):
                cur_iter["u"] = u
                emit_E(u)
                if u + 2 < NI:
                    emit_S(u + 2)
                while late:
                    emit_PV(late.pop(0))
                emit_PV(u)
            while late:
                emit_PV(late.pop(0))
            while pending_stats:
                flush_stats(1, None)

            rstd_from_ss(stat[:, 16:24], res("gss"), 512, 8)
            if dbga_d is not None:
                P.op("pool", lambda e: e.dma_start(out=dbga_d, in_=attnT[:]), reads=[res("attnT%d" % c) for c in range(8)],
                     dma=True, sem_res=res("dbga"))
            so_slots = [page_acquire(PG_O + k) for k in range(4)]
            MS = [(mixed, res("mixed")), (qtok, res("qtok"))]
            MCOL = [1, 4]
            FCOL = [2, 5]

            def e_proj(tt, so_slots=so_slots):
                base = 4 * (tt % 2)
                for grp in range(2):
                    for k in range(4):
                        b = base + 2 * grp + k // 2
                        for c in range(4):
                            cc = 4 * grp + c
                            P.op("pe", lambda e, b=b, k=k, cc=cc, c=c: e.matmul(
                                pb[b][:, (k % 2) * 256:(k % 2 + 1) * 256], lhsT=attnT[:, cc, tt * 128:(tt + 1) * 128],
                                rhs=pages[:, so_slots[k], cc * 256:(cc + 1) * 256], start=(c == 0), stop=(c == 3)),
                                reads=[res("attnT%d" % cc), slot_res[so_slots[k]]], writes=[BK[b]])

            def e_mix(tt):
                base = 4 * (tt % 2)
                mx, mr = MS[tt % 2]
                mx3 = mx[:].rearrange("p (a c) -> p a c", a=2)
                P.op("dve", lambda e: e.tensor_scalar(
                    out=mx3, in0=pball[:, base:base + 2, :], scalar1=stat[:, 16 + tt:17 + tt],
                    scalar2=None, op0=ALU.mult), reads=[BK[base], BK[base + 1], res("gss")], writes=[mr])
                P.op("dve", lambda e: e.scalar_tensor_tensor(
                    out=mx3, in0=pball[:, base + 2:base + 4, :], scalar=stat[:, 20 + tt:21 + tt],
                    in1=mx3, op0=ALU.mult, op1=ALU.add),
                    reads=[BK[base + 2], BK[base + 3], res("gss"), mr], writes=[mr])

            def e_ss1(tt):
                mx, mr = MS[tt % 2]
                col = MCOL[tt % 2]
                ssr = res("ss_col%d" % col)
                P.op("act", lambda e: e.activation(out=junk[:], in_=mx[:], func=AF.Square, accum_out=stat[:, col:col + 1]),
                     reads=[mr], writes=[res("PT0"), ssr])
                rstd_from_ss(stat[:, col:col + 1], ssr, D)

            def e_res(tt):
                mx, mr = MS[tt % 2]
                col = MCOL[tt % 2]
                xr = res("xblk%d" % tt)
                P.op("dve", lambda e: e.scalar_tensor_tensor(out=mx[:], in0=mx[:], scalar=stat[:, col:col + 1],
                                                             in1=gains[:, 1, :], op0=ALU.mult, op1=ALU.mult),
                     reads=[mr, res("ss_col%d" % col), res("gains")], writes=[mr])
                P.op("pool", lambda e: e.tensor_tensor(out=xblk[:, tt, :], in0=xblk[:, tt, :], in1=mx[:],
                                                       op=ALU.add), reads=[xr, mr], writes=[xr])
                if dbg_d is not None:
                    P.op("sp", lambda e, i=t0 + tt: e.dma_start(out=dbg_d[i * 128:(i + 1) * 128, :], in_=xblk[:, tt, :]),
                         reads=[xr], dma=True, sem_res=res("dbg%d" % tt))

            def e_ss2(tt):
                xr = res("xblk%d" % tt)
                col = FCOL[tt % 2]
                ssr = res("ss_col%d" % col)
                P.op("act", lambda e: e.activation(out=junk[:], in_=xblk[:, tt, :], func=AF.Square,
                                                   accum_out=stat[:, col:col + 1]),
                     reads=[xr], writes=[res("PT0"), ssr])
                rstd_from_ss(stat[:, col:col + 1], ssr, D)

            def e_h2(tt):
                xr = res("xblk%d" % tt)
                col = FCOL[tt % 2]
                sl = tt % 2
                P.op("dve", lambda e: e.scalar_tensor_tensor(out=xn[:, sl, :], in0=xblk[:, tt, :], scalar=stat[:, col:col + 1],
                                                             in1=gains[:, 2, :], op0=ALU.mult, op1=ALU.mult),
                     reads=[xr, res("ss_col%d" % col), res("gains")], writes=[res("xn%d" % sl)])

            def e_h2T(tt):
                sl = tt % 2
                tb = [(4, 5), (6, 7)][tt % 2]
                transposes_to(lambda lo, hi: hT[:, lo:hi, tt * 128:(tt + 1) * 128],
                              lambda c: xn[:, sl, c * 128:(c + 1) * 128], res("xn%d" % sl), 8, tb, res("hT%d" % tt), "f")

            pipeline([0, 1, 2, 3], [e_proj, e_mix, e_ss1, e_res, e_ss2, e_h2, e_h2T])
            for s_ in so_slots:
                page_load(s_)

            hT_all = [res("hT%d" % tt) for tt in range(4)]
            for k in range(11):
                sg_ = page_acquire(PG_G + k)
                su_ = page_acquire(PG_U + k)
                for fl in range(2):
                    fc = 2 * k + fl
                    gbk, ubk = (0, 1) if fc % 2 == 0 else (2, 3)
                    for kc in range(8):
                        P.op("pe", lambda e, kc=kc, fl=fl, sg_=sg_, gbk=gbk: e.matmul(
                            pb[gbk], lhsT=pages[:, sg_, kc * 256 + fl * 128:kc * 256 + (fl + 1) * 128],
                            rhs=hT[:, kc, :], start=(kc == 0), stop=(kc == 7)),
                            reads=hT_all + [slot_res[sg_]], writes=[BK[gbk]])
                    for kc in range(8):
                        P.op("pe", lambda e, kc=kc, fl=fl, su_=su_, ubk=ubk: e.matmul(
                            pb[ubk], lhsT=pages[:, su_, kc * 256 + fl * 128:kc * 256 + (fl + 1) * 128],
                            rhs=hT[:, kc, :], start=(kc == 0), stop=(kc == 7)),
                            reads=hT_all + [slot_res[su_]], writes=[BK[ubk]])
                    sgr = res("sg%d" % (fc % 2))
                    P.op("act", lambda e, fc=fc, gbk=gbk: e.activation(out=sg[:, fc % 2, :], in_=pb[gbk], func=AF.Silu),
                         reads=[BK[gbk]], writes=[sgr])
                    P.op("dve", lambda e, fc=fc, ubk=ubk: e.tensor_tensor(out=aT[:, fc, :], in0=sg[:, fc % 2, :],
                                                                          in1=pb[ubk], op=ALU.mult),
                         reads=[sgr, BK[ubk]], writes=[res("aT")])
                page_load(sg_)
                page_load(su_)
            ssf = res("ss_f")
            P.op("act", lambda e: e.activation(out=stat[:, 60:61], in_=epsc[:, 0:1], func=AF.Ln),
                 reads=[res("epsc")], writes=[res("tblwarm")])
            for cb in range(2):
                for grp in range(6):
                    sd_ = page_acquire(PG_D + cb * 6 + grp)
                    nfc = min(4, NFC - 4 * grp)
                    for tt in range(4):
                        bk = 4 * (1 - cb) + tt
                        for f4 in range(nfc):
                            fc = 4 * grp + f4
                            P.op("pe", lambda e, tt=tt, f4=f4, fc=fc, sd_=sd_, bk=bk: e.matmul(
                                pb[bk], lhsT=aT[:, fc, tt * 128:(tt + 1) * 128],
                                rhs=pages[:, sd_, f4 * 512:(f4 + 1) * 512], start=(fc == 0), stop=(fc == NFC - 1)),
                                reads=[res("aT"), slot_res[sd_]], writes=[BK[bk]])
                    page_load(sd_)
                for tt in range(4):
                    bk = 4 * (1 - cb) + tt
                    P.op("act", lambda e, tt=tt, bk=bk, cb=cb: e.activation(
                        out=junk[:, 0:512], in_=pb[bk], func=AF.Square, accum_out=stat[:, 32 + 4 * cb + tt:33 + 4 * cb + tt]),
                        reads=[BK[bk]], writes=[res("PT0"), ssf])
            P.op("dve", lambda e: e.tensor_tensor(out=stat[:, 32:36], in0=stat[:, 32:36], in1=stat[:, 36:40],
                                                  op=ALU.add), reads=[ssf], writes=[ssf])
            rstd_from_ss(stat[:, 32:36], ssf, D, 4)
            for tt in range(4):
                mx, mr = MS[tt % 2]
                for cb in range(2):
                    bk = 4 * (1 - cb) + tt
                    P.op("dve", lambda e, tt=tt, cb=cb, bk=bk, mx=mx: e.scalar_tensor_tensor(
                        out=mx[:, cb * 512:(cb + 1) * 512], in0=pb[bk], scalar=stat[:, 32 + tt:33 + tt],
                        in1=gains[:, 3, cb * 512:(cb + 1) * 512], op0=ALU.mult, op1=ALU.mult),
                        reads=[BK[bk], ssf, res("gains")], writes=[mr])
                xr = res("xblk%d" % tt)
                P.op("pool" if tt % 2 == 0 else "dve", lambda e, tt=tt, mx=mx: e.tensor_tensor(
                    out=xblk[:, tt, :], in0=xblk[:, tt, :], in1=mx[:], op=ALU.add), reads=[xr, mr], writes=[xr])
                i = t0 + tt
                P.op("sp", lambda e, tt=tt, i=i: e.dma_start(out=out_d[i * 128:(i + 1) * 128, :], in_=xblk[:, tt, :]),
                     reads=[xr], dma=True, sem_res=res("ost%d" % tt))
                if bi_ + 1 < len(blocks_) and tt >= 1:
                    st_load_x(4 * blocks_[bi_ + 1] + tt - 1)
                    st_load_rope(4 * blocks_[bi_ + 1] + tt - 1)
            if bi_ + 1 < len(blocks_):
                st_load_x(4 * blocks_[bi_ + 1] + 3)
                st_load_rope(4 * blocks_[bi_ + 1] + 3)
                preloaded["x"] = True
        stats = P.emit()
    return nc, stats


def _rope_table(S):
    t = np.arange(S)
    row = (t // 64).astype(np.float32)
    col = (t % 64).astype(np.float32)
    f_ax = (np.float32(THETA) ** (-(np.arange(16, dtype=np.float32) / np.float32(16)))).astype(np.float32)
    ang_ax = np.concatenate([row[:, None] * f_ax[None, :], col[:, None] * f_ax[None, :]], axis=-1).astype(np.float32)
    f_1d = (np.float32(THETA) ** (-(np.arange(32, dtype=np.float32) / np.float32(32)))).astype(np.float32)
    ang_1d = (t.astype(np.float32)[:, None] * f_1d[None, :]).astype(np.float32)
    tabs = []
    for ang in (ang_ax, ang_1d):
        c = np.cos(ang).astype(np.float32)
        s = np.sin(ang).astype(np.float32)
        tabs.append(np.concatenate([c, c], axis=-1))
        tabs.append(np.concatenate([-s, s], axis=-1))
    return np.ascontiguousarray(np.concatenate(tabs, axis=-1), dtype=np.float32)


def _prep_shared(inp, S):
    w_in = np.asarray(inp["w_in"], np.float32)[0]
    qA, kA, vA = w_in[:, 0:512], w_in[:, 512:640], w_in[:, 640:768]
    qB, kB, vB = w_in[:, 768:1280], w_in[:, 1280:1408], w_in[:, 1408:1536]

    def pair(q):
        return np.concatenate([np.concatenate([q[:, g * 64:(g + 1) * 64], q[:, (4 + g) * 64:(5 + g) * 64]], 1)
                               for g in range(4)], 1)

    w_kv = np.concatenate([kA, kB, vA, vB], 1)
    w_q = np.concatenate([pair(qA), pair(qB)], 1)
    w_out = np.asarray(inp["w_out"], np.float32)[0]
    rows = []
    for c in range(4):
        rows += list(range(c * 64, (c + 1) * 64)) + list(range((4 + c) * 64, (5 + c) * 64))
    for c in range(4):
        rows += list(range(512 + (4 + c) * 64, 512 + (5 + c) * 64)) + list(range(512 + c * 64, 512 + (c + 1) * 64))
    w_o = w_out[np.array(rows)]
    ga = np.asarray(inp["group_norm_a"], np.float32)[0]
    gb = np.asarray(inp["group_norm_b"], np.float32)[0]
    gcat = np.concatenate([ga, gb])
    gcol = gcat[np.array(rows)].reshape(8, 128).T
    qn = np.asarray(inp["q_norm_a"], np.float32)[0]
    kn = np.asarray(inp["k_norm_a"], np.float32)[0]
    sw_ = lambda g: np.concatenate([g[32:], g[:32]])
    qk = np.stack([qn, sw_(qn), kn, sw_(kn)], 0)
    gains = np.stack([np.asarray(inp[k], np.float32)[0] for k in
                      ("norm_mix_pre", "norm_mix_post", "norm_ffn_pre", "norm_ffn_post")], 0)
    kk = np.arange(128)[:, None]
    qq = np.arange(128)[None, :]
    valid = np.concatenate([(qq <= kk), np.ones((128, 128), bool), (kk <= qq)], 1)
    mask = np.where(valid, 0.0, -30000.0).astype(np.float32)
    c = np.ascontiguousarray
    return {
        "w_kv": c(w_kv), "w_q": c(w_q), "w_o": c(w_o),
        "w_g": c(np.asarray(inp["w_gate"], np.float32)[0]), "w_u": c(np.asarray(inp["w_up"], np.float32)[0]),
        "w_d": c(np.asarray(inp["w_down"], np.float32)[0]),
        "gains": c(gains), "qk_gain": c(qk), "sink": c(np.asarray(inp["sink_b"], np.float32).reshape(1, 8)),
        "gcol": c(gcol.astype(np.float32)), "rope": _rope_table(S), "ident": np.eye(128, dtype=np.float32),
        "mask": c(mask),
    }


_CACHE = {}


def kernel(**inputs):
    x = np.asarray(inputs["x"], np.float32)
    B, S, _ = x.shape
    shared = _prep_shared(inputs, S)
    if S not in _CACHE:
        _CACHE[S] = build_program(S)[0]
    nc = _CACHE[S]
    in_maps = []
    for b in range(B):
        m = dict(shared)
        m["x"] = np.ascontiguousarray(x[b])
        in_maps.append(m)
    res = run_bass_kernel_spmd(nc, in_maps, core_ids=list(range(B)))
    return np.stack([np.asarray(r["out"], np.float32) for r in res.results], 0)
```

```python
import numpy as np
from contextlib import ExitStack
import concourse.bass as bass
import concourse.mybir as mybir
from concourse.bass_utils import run_bass_kernel_spmd

F32 = mybir.dt.float32
BF16 = mybir.dt.bfloat16
AF = mybir.ActivationFunctionType
ALU = mybir.AluOpType
AX = mybir.AxisListType

D = 1024
DFF = 2816
NFC = DFF // 128
EPS = 1e-6
THETA = 10000.0

COMPUTE = ("pe", "act", "dve", "pool")
ALLQ = ("pe", "act", "dve", "pool", "sp")


class Res:
    __slots__ = ("name", "last_w", "rd_eng", "rd_dma", "sem", "dma_cnt", "excl", "last_real")

    def __init__(self, name, excl=False):
        self.name = name
        self.last_w = None
        self.rd_eng = {}
        self.rd_dma = []
        self.sem = None
        self.dma_cnt = 0
        self.excl = excl
        self.last_real = True


class Op:
    __slots__ = ("eng", "fn", "deps", "dma", "signal", "sig_val", "sem_res")

    def __init__(self, eng, fn, dma, sem_res):
        self.eng = eng
        self.fn = fn
        self.deps = []
        self.dma = dma
        self.signal = dma
        self.sig_val = 0
        self.sem_res = sem_res


class Prog:
    def __init__(self, nc, stack):
        self.nc = nc
        self.stack = stack
        self.ops = []
        self.dma_res = []

    def op(self, eng, fn, reads=(), writes=(), dma=False, sem_res=None):
        o = Op(eng, fn, dma, sem_res)
        deps = {}
        xreads = [r for r in reads if r.excl]
        reads = [r for r in reads if not r.excl]
        for r in xreads:
            p = r.last_w
            if p is not None and not (p.eng == eng and not r.last_real and not p.dma):
                deps[id(p)] = p
        for r in reads:
            if r.last_w is not None:
                deps[id(r.last_w)] = r.last_w
        for w in writes:
            if w.last_w is not None:
                deps[id(w.last_w)] = w.last_w
            for p in w.rd_eng.values():
                deps[id(p)] = p
            for p in w.rd_dma:
                deps[id(p)] = p
        for p in deps.values():
            if p.eng == "pe" and eng == "pe" and not p.dma and not dma:
                continue
            p.signal = True
            o.deps.append(p)
        for r in reads:
            if dma:
                r.rd_dma.append(o)
            else:
                r.rd_eng[eng] = o
        for w in writes:
            w.last_w = o
            w.last_real = True
            w.rd_eng = {}
            w.rd_dma = []
        for r in xreads:
            if r.last_w is not o:
                r.last_w = o
                r.last_real = False
        if dma:
            if sem_res.sem is None:
                sem_res.sem = self.stack.enter_context(self.nc.semaphore("d_" + sem_res.name))
                self.dma_res.append(sem_res)
            sem_res.dma_cnt += 1
            o.sig_val = 16 * sem_res.dma_cnt
        self.ops.append(o)
        return o

    def emit(self):
        nc = self.nc
        esem = {e: self.stack.enter_context(nc.semaphore("e_" + e)) for e in COMPUTE}
        cnt = {e: 0 for e in COMPUTE}
        if _os.environ.get("KALLSIG"):
            for o in self.ops:
                if o.eng in _os.environ["KALLSIG"].split(","):
                    o.signal = True
        for o in self.ops:
            if not o.dma and o.signal:
                cnt[o.eng] += 1
                o.sig_val = cnt[o.eng]
        per = {e: [] for e in ALLQ}
        for o in self.ops:
            per[o.eng].append(o)
        finals = [(r.sem, 16 * r.dma_cnt) for r in self.dma_res]

        def run(eng_name, eng):
            seen = {}
            for o in per[eng_name]:
                for p in o.deps:
                    if p.dma:
                        s, v = p.sem_res.sem, p.sig_val
                    else:
                        s, v = esem[p.eng], p.sig_val
                    k = id(s)
                    if seen.get(k, 0) < v:
                        eng.wait_ge(s, v)
                        seen[k] = v
                ins = o.fn(eng)
                if o.dma:
                    ins.then_inc(o.sem_res.sem, 16)
                elif o.signal:
                    ins.then_inc(esem[o.eng], 1)
            if eng_name == "sp":
                for s, v in finals:
                    eng.wait_ge(s, v)

        with nc.Block() as block:
            @block.sync
            def _(e):
                run("sp", e)

            @block.tensor
            def _(e):
                run("pe", e)

            @block.scalar
            def _(e):
                run("act", e)

            @block.vector
            def _(e):
                run("dve", e)

            @block.gpsimd
            def _(e):
                run("pool", e)
        return {e: len(per[e]) for e in ALLQ}, cnt


PG_KV = 0
PG_Q = 2
PG_O = 6
PG_G = 10
PG_U = 21
PG_D = 32
NPAGES = 44
NSLOT = 8


import os as _os


def DBG_BLOCKS(NB):
    v = _os.environ.get('KBLOCKS')
    return range(NB) if v is None else [int(t) for t in v.split(',')]


def build_program(S):
    NT = S // 128
    NB = S // 512
    nc = bass.Bass("TRN2", target_bir_lowering=False)

    def din(name, shape):
        return nc.dram_tensor(name, list(shape), F32, kind="ExternalInput").ap()

    x_d = din("x", [S, D])
    wkv_d = din("w_kv", [D, 512])
    wq_d = din("w_q", [D, 1024])
    wo_d = din("w_o", [D, 1024])
    wg_d = din("w_g", [D, DFF])
    wu_d = din("w_u", [D, DFF])
    wd_d = din("w_d", [DFF, D])
    gains_d = din("gains", [4, D])
    qk_d = din("qk_gain", [4, 64])
    sink_d = din("sink", [1, 8])
    gcol_d = din("gcol", [128, 8])
    rope_d = din("rope", [S, 256])
    ident_d = din("ident", [128, 128])
    mask_d = din("mask", [128, 384])
    out_d = nc.dram_tensor("out", [S, D], F32, kind="ExternalOutput").ap()
    dbg_d = nc.dram_tensor("dbg", [S, D], F32, kind="ExternalOutput").ap() if _os.environ.get("KDBG") else None
    scr = nc.dram_tensor("wscr", [NPAGES, 128, 2048], BF16).ap()
    dbga_d = nc.dram_tensor("dbga", [128, 8, 512], F32, kind="ExternalOutput").ap() if _os.environ.get("KDBGA") else None

    with ExitStack() as st:
        def sb(name, shape, dt):
            return st.enter_context(nc.sbuf_tensor("s_" + name, list(shape), dt))

        pages = sb("pages", [128, NSLOT, 2048], BF16)
        kT = sb("kT", [128, 2, S], BF16)
        V = sb("V", [128, NT, 6, 64], BF16)
        gains = sb("gains", [128, 4, D], F32)
        qkg = sb("qkg", [128, 4, 64], F32)
        es = sb("es", [128, 8], F32)
        gcol = sb("gcol", [128, 8], F32)
        identb = sb("identb", [128, 128], BF16)
        maskb = sb("maskb", [128, 384], BF16)
        ones_c = sb("ones_c", [128, 8], BF16)
        epsc = sb("epsc", [128, 8], F32)
        xblk = sb("xblk", [128, 4, D], F32)
        ropet = sb("ropet", [128, 4, 256], F32)
        tabs = sb("tabs", [128, 2, 2, 64], F32)
        xn = sb("xn", [128, 2, D], BF16)
        hT = sb("hT", [128, 8, 512], BF16)
        qtok = sb("qtok", [128, D], F32)
        tq = sb("tq", [128, 512], F32)
        sw = sb("sw", [128, 512], F32)
        qrot = sb("qrot", [128, 2, D], BF16)
        qT = sb("qT", [128, 8, 512], BF16)
        PT = sb("PT", [128, 3, 1024], BF16)
        PTW = sb("PTW", [128, 1, 768], BF16)
        junk = PT[:, 0, :]
        attnT = sb("attnT", [128, 8, 512], BF16)
        ro = sb("ro", [128, 5, 512], F32)
        rec = ro[:, 0:2, :]
        onrm = ro[:, 2:5, :]
        sqb = sb("sqb", [128, 8, 512], BF16)
        mixed = sb("mixed", [128, D], F32)
        aT = sb("aT", [128, NFC, 512], BF16)
        sg = sb("sg", [128, 2, 512], F32)
        stat = sb("stat", [128, 64], F32)

        pball = st.enter_context(nc.psum_tensor("pball", [128, 8, 512], F32))
        pb = [pball[:, i, :] for i in range(8)]
        pp = [st.enter_context(nc.psum_tensor("pp%d" % i, [128, 1024], F32)) for i in range(0)]

        P = Prog(nc, st)
        R = {}

        def res(name, excl=False):
            if name not in R:
                R[name] = Res(name, excl)
            return R[name]

        BK = [res("bank%d" % i, excl=True) for i in range(8)]
        scr_res = [res("scr%d" % i) for i in range(NPAGES)]

        def cast_page(pg, src_ap):
            P.op("pool", lambda e: e.dma_start(out=scr[pg].rearrange("p (a f) -> p a f", a=src_ap.shape[1]),
                                               in_=src_ap),
                 writes=[scr_res[pg]], dma=True, sem_res=scr_res[pg])

        wkv_v = wkv_d.rearrange("(kc p) f -> p kc f", p=128)
        wq_v = wq_d.rearrange("(kc p) f -> p kc f", p=128)
        wo_v = wo_d.rearrange("(kc p) f -> p kc f", p=128)
        wg_v = wg_d.rearrange("(kc p) f -> p kc f", p=128)
        wu_v = wu_d.rearrange("(kc p) f -> p kc f", p=128)
        wd_v = wd_d.rearrange("(fc p) d -> p fc d", p=128)
        for k in range(2):
            cast_page(PG_KV + k, wkv_v[:, :, k * 256:(k + 1) * 256])
        cres = res("consts")
        P.op("sp", lambda e: e.dma_start(out=gains[:].rearrange("p a d -> p (a d)"),
                                         in_=gains_d.rearrange("a d -> (a d)").partition_broadcast(128)),
             writes=[res("gains")], dma=True, sem_res=res("gains"))
        P.op("sp", lambda e: e.dma_start(out=qkg[:].rearrange("p a d -> p (a d)"),
                                         in_=qk_d.rearrange("a d -> (a d)").partition_broadcast(128)),
             writes=[res("qkg")], dma=True, sem_res=res("qkg"))
        P.op("sp", lambda e: e.dma_start(out=es[:], in_=sink_d.rearrange("a d -> (a d)").partition_broadcast(128)),
             writes=[res("es")], dma=True, sem_res=res("es"))
        P.op("sp", lambda e: e.dma_start(out=gcol[:], in_=gcol_d), writes=[res("gcol")], dma=True, sem_res=res("gcol"))
        P.op("pool", lambda e: e.dma_start(out=identb[:], in_=ident_d), writes=[res("identb")], dma=True,
             sem_res=res("identb"))
        P.op("pool", lambda e: e.dma_start(out=maskb[:], in_=mask_d), writes=[res("maskb")], dma=True,
             sem_res=res("maskb"))
        for k in range(4):
            cast_page(PG_Q + k, wq_v[:, :, k * 256:(k + 1) * 256])
        for k in range(4):
            cast_page(PG_O + k, wo_v[:, :, k * 256:(k + 1) * 256])
        for k in range(11):
            cast_page(PG_G + k, wg_v[:, :, k * 256:(k + 1) * 256])
            cast_page(PG_U + k, wu_v[:, :, k * 256:(k + 1) * 256])
        for cb in range(2):
            for grp in range(6):
                nfc = min(4, NFC - 4 * grp)
                pg = PG_D + cb * 6 + grp
                P.op("pool", lambda e, pg=pg, grp=grp, nfc=nfc, cb=cb: e.dma_start(
                    out=scr[pg].rearrange("p (a f) -> p a f", a=4)[:, 0:nfc, :],
                    in_=wd_v[:, 4 * grp:4 * grp + nfc, cb * 512:(cb + 1) * 512]),
                    writes=[scr_res[pg]], dma=True, sem_res=scr_res[pg])
        P.op("act", lambda e: e.activation(out=es[:], in_=es[:], func=AF.Exp), reads=[res("es")], writes=[res("es")])
        P.op("pool", lambda e: e.memset(V[:, :, 1, :], 1.0), writes=[res("Vones")])
        P.op("pool", lambda e: e.memset(V[:, :, 4, :], 1.0), writes=[res("Vones")])
        P.op("pool", lambda e: e.memset(ones_c[:], 1.0), writes=[res("ones_c")])
        P.op("pool", lambda e: e.memset(epsc[:], EPS), writes=[res("epsc")])

        sched = [PG_KV, PG_KV + 1]
        for j in DBG_BLOCKS(NB):
            sched += [PG_Q + k for k in range(4)]
            sched += [PG_O + k for k in range(4)]
            for k in range(11):
                sched += [PG_G + k, PG_U + k]
            sched += [PG_D + k for k in range(12)]
        slot_res = [res("slot%d" % s) for s in range(NSLOT)]
        pstate = {"next": 0, "fifo": []}

        def page_load(slot):
            i = pstate["next"]
            if i >= len(sched):
                return
            pstate["next"] = i + 1
            pg = sched[i]
            w = 1024 if pg in (PG_D + 5, PG_D + 11) else 2048
            P.op("sp", lambda e, slot=slot, pg=pg, w=w: e.dma_start(out=pages[:, slot, 0:w], in_=scr[pg][:, 0:w]),
                 reads=[scr_res[pg]], writes=[slot_res[slot]], dma=True, sem_res=slot_res[slot])
            pstate["fifo"].append((slot, pg))

        def page_acquire(expect):
            slot, pg = pstate["fifo"].pop(0)
            assert pg == expect, (pg, expect)
            return slot

        for s in range(NSLOT):
            page_load(s)

        def rstd_from_ss(ss_ap, ss_res, n, k=1):
            P.op("act", lambda e: e.activation(out=ss_ap, in_=ss_ap, func=AF.Ln, scale=1.0 / n, bias=epsc[:, 0:1]),
                 reads=[ss_res, res("epsc")], writes=[ss_res])
            P.op("act", lambda e: e.activation(out=ss_ap, in_=ss_ap, func=AF.Exp, scale=-0.5),
                 reads=[ss_res], writes=[ss_res])

        def transposes_to(dst_fn, src_ap_fn, src_res, n, banks, dst_res, tag):
            for c in range(n):
                b = banks[c // 4]
                P.op("pe", lambda e, c=c, b=b: e.matmul(pb[b][:, (c % 4) * 128:(c % 4 + 1) * 128],
                                                         lhsT=src_ap_fn(c), rhs=identb[:], start=True, stop=True),
                     reads=[src_res, res("identb")], writes=[BK[b]])
            if n == 8 and banks[1] == banks[0] + 1:
                b0 = banks[0]
                P.op("act", lambda e, b0=b0: e.activation(
                    out=dst_fn(0, 8), in_=pball[:, b0:b0 + 2, :].rearrange("p a (c t) -> p (a c) t", t=128),
                    func=AF.Copy), reads=[BK[b0], BK[b0 + 1]], writes=[dst_res])
                return
            for h in range((n + 3) // 4):
                b = banks[h]
                lo, hi = 4 * h, min(4 * h + 4, n)
                P.op("act", lambda e, b=b, lo=lo, hi=hi: e.activation(
                    out=dst_fn(lo, hi), in_=pb[b][:, 0:(hi - lo) * 128].rearrange("p (c t) -> p c t", t=128),
                    func=AF.Copy), reads=[BK[b]], writes=[dst_res])

        def norm_tile(x_ap, x_res, gain_idx, xn_slot, ss_col, tag):
            ssr = res("ss_col%d" % ss_col)
            ss_ap = stat[:, ss_col:ss_col + 1]
            P.op("act", lambda e: e.activation(out=junk[:], in_=x_ap, func=AF.Square, accum_out=ss_ap),
                 reads=[x_res], writes=[res("PT0"), ssr])
            rstd_from_ss(ss_ap, ssr, D)
            P.op("dve", lambda e: e.scalar_tensor_tensor(out=xn[:, xn_slot, :], in0=x_ap, scalar=ss_ap,
                                                         in1=gains[:, gain_idx, :], op0=ALU.mult, op1=ALU.mult),
                 reads=[x_res, ssr, res("gains")], writes=[res("xn%d" % xn_slot)])

        def rope(dst_ap, src_ap, H, c_ap, s_ap, reads, dst_res, eng2="pool"):
            s3 = src_ap.rearrange("p (h d) -> p h d", d=64)
            t3 = tq[:, 0:H * 64].rearrange("p (h d) -> p h d", d=64)
            w3 = sw[:, 0:H * 64].rearrange("p (h d) -> p h d", d=64)
            d3 = dst_ap.rearrange("p (h d) -> p h d", d=64)
            cb_ = c_ap.unsqueeze(1).to_broadcast([128, H, 64])
            P.op("dve", lambda e: e.tensor_tensor(out=t3, in0=s3, in1=cb_, op=ALU.mult),
                 reads=reads, writes=[res("tq")])
            P.op(eng2, lambda e: e.tensor_tensor(out=w3[:, :, 0:32], in0=s3[:, :, 32:64],
                                                 in1=s_ap[:, 0:32].unsqueeze(1).to_broadcast([128, H, 32]), op=ALU.mult),
                 reads=reads, writes=[res("sw")])
            P.op(eng2, lambda e: e.tensor_tensor(out=w3[:, :, 32:64], in0=s3[:, :, 0:32],
                                                 in1=s_ap[:, 32:64].unsqueeze(1).to_broadcast([128, H, 32]), op=ALU.mult),
                 reads=reads, writes=[res("sw")])
            P.op("dve", lambda e: e.tensor_tensor(out=d3, in0=t3, in1=w3, op=ALU.add),
                 reads=[res("tq"), res("sw")], writes=[dst_res])

        def headnorm(src_ap, src_res, H, ss_col, tag):
            ssr = res("hss")
            ss_ap = stat[:, ss_col:ss_col + H]
            s3 = src_ap.rearrange("p (h d) -> p h d", d=64)
            P.op("act", lambda e: e.activation(out=sw[:, 0:H * 64], in_=src_ap, func=AF.Square),
                 reads=[src_res], writes=[res("sw")])
            P.op("dve", lambda e: e.tensor_reduce(out=ss_ap, in_=sw[:, 0:H * 64].rearrange("p (h d) -> p h d", d=64),
                                                  axis=AX.X, op=ALU.add), reads=[res("sw")], writes=[ssr])
            rstd_from_ss(ss_ap, ssr, 64)
            P.op("dve", lambda e: e.tensor_tensor(out=s3, in0=s3, in1=ss_ap.unsqueeze(2).to_broadcast([128, H, 64]),
                                                  op=ALU.mult), reads=[src_res, ssr], writes=[src_res])

        def fold_tables(slot, rope_ap, gidx):
            tr = res("tabs%d" % slot)
            P.op("pool", lambda e: e.tensor_tensor(out=tabs[:, slot, 0, :], in0=rope_ap[:, 0:64], in1=qkg[:, gidx, :],
                                                   op=ALU.mult), reads=[res("ropet"), res("qkg")], writes=[tr])
            P.op("pool", lambda e: e.tensor_tensor(out=tabs[:, slot, 1, :], in0=rope_ap[:, 64:128],
                                                   in1=qkg[:, gidx + 1, :], op=ALU.mult),
                 reads=[res("ropet"), res("qkg")], writes=[tr])
            return tr

        sgflat = sg[:].rearrange("p a c -> p (a c)")
        P1Q = [(qtok[:, k * 256:(k + 1) * 256], [res("qtok_p1_%d" % k)]) for k in range(4)]
        QS = [(qtok, [res("qtok")] + [res("qtok_p1_%d" % k) for k in range(4)]), (sgflat, [res("sg0"), res("sg1")])]
        NCOL = [0, 3]
        HCOL = [8, 24]
        hsq = mixed[:, 0:512]

        def pipeline(tiles, stages):
            n, ns = len(tiles), len(stages)
            for step in range(n + ns - 1):
                for k in reversed(range(ns)):
                    t = step - k
                    if 0 <= t < n:
                        stages[k](tiles[t])

        def st_load_x(i):
            tt = i % 4
            xr = res("xblk%d" % tt)
            P.op("sp", lambda e: e.dma_start(out=xblk[:, tt, :], in_=x_d[i * 128:(i + 1) * 128, :]),
                 writes=[xr], dma=True, sem_res=xr)

        def st_load_rope(i):
            tt = i % 4
            rr_ = res("ropet%d" % tt)
            P.op("sp", lambda e: e.dma_start(out=ropet[:, tt, :], in_=rope_d[i * 128:(i + 1) * 128, :]),
                 writes=[rr_], dma=True, sem_res=rr_)

        def st_ss(i):
            tt = i % 4
            xr = res("xblk%d" % tt)
            col = NCOL[i % 2]
            ssr = res("ss_col%d" % col)
            ss_ap = stat[:, col:col + 1]
            P.op("act", lambda e: e.activation(out=junk[:], in_=xblk[:, tt, :], func=AF.Square, accum_out=ss_ap),
                 reads=[xr], writes=[res("PT0"), ssr])
            rstd_from_ss(ss_ap, ssr, D)

        def st_xn(i):
            tt = i % 4
            xr = res("xblk%d" % tt)
            col = NCOL[i % 2]
            sl = i % 2
            P.op("dve", lambda e: e.scalar_tensor_tensor(out=xn[:, sl, :], in0=xblk[:, tt, :], scalar=stat[:, col:col + 1],
                                                         in1=gains[:, 0, :], op0=ALU.mult, op1=ALU.mult),
                 reads=[xr, res("ss_col%d" % col), res("gains")], writes=[res("xn%d" % sl)])

        def st_T(i, tb):
            tt = i % 4
            sl = i % 2
            transposes_to(lambda lo, hi: hT[:, lo:hi, tt * 128:(tt + 1) * 128],
                          lambda c: xn[:, sl, c * 128:(c + 1) * 128], res("xn%d" % sl), 8, tb, res("hT%d" % tt), "x")

        def st_xn_T(i, tb):
            tt = i % 4
            xr = res("xblk%d" % tt)
            col = NCOL[i % 2]
            sl = i % 2
            P.op("dve", lambda e: e.scalar_tensor_tensor(out=xn[:, sl, :], in0=xblk[:, tt, :], scalar=stat[:, col:col + 1],
                                                         in1=gains[:, 0, :], op0=ALU.mult, op1=ALU.mult),
                 reads=[xr, res("ss_col%d" % col), res("gains")], writes=[res("xn%d" % sl)])
            transposes_to(lambda lo, hi: hT[:, lo:hi, tt * 128:(tt + 1) * 128],
                          lambda c: xn[:, sl, c * 128:(c + 1) * 128], res("xn%d" % sl), 8, tb, res("hT%d" % tt), "x")

        def st_headnorm(i, H, gidx, part=None, slots=None):
            qs, qres = (slots[i % len(slots)] if slots is not None else QS[i % 2])
            tt = i % 4
            hc = HCOL[i % 2]
            ssr = res("hss%d" % (i % 2))
            ss_ap = stat[:, hc:hc + H]
            src = qs[:, 0:H * 64]
            s3 = src.rearrange("p (h d) -> p h d", d=64)
            if part in (None, "a"):
                P.op("act", lambda e: e.activation(out=hsq[:, 0:H * 64], in_=src, func=AF.Square),
                     reads=qres, writes=[res("mixed")])
                P.op("dve", lambda e: e.tensor_reduce(out=ss_ap, in_=hsq[:, 0:H * 64].rearrange("p (h d) -> p h d", d=64),
                                                      axis=AX.X, op=ALU.add), reads=[res("mixed")], writes=[ssr])
            if part == "a":
                return
            rstd_from_ss(ss_ap, ssr, 64)
            P.op("dve", lambda e: e.tensor_tensor(out=s3, in0=s3, in1=ss_ap.unsqueeze(2).to_broadcast([128, H, 64]),
                                                  op=ALU.mult), reads=qres + [ssr], writes=qres)
            tr = res("tabs%d" % (i % 2))
            rr_ = res("ropet%d" % tt)
            P.op("pool", lambda e: e.tensor_tensor(out=tabs[:, i % 2, 0, :], in0=ropet[:, tt, 0:64], in1=qkg[:, gidx, :],
                                                   op=ALU.mult), reads=[rr_, res("qkg")], writes=[tr])
            P.op("pool", lambda e: e.tensor_tensor(out=tabs[:, i % 2, 1, :], in0=ropet[:, tt, 64:128],
                                                   in1=qkg[:, gidx + 1, :], op=ALU.mult),
                 reads=[rr_, res("qkg")], writes=[tr])

        def st_rope(i, H, slots=None):
            qs, qres = (slots[i % len(slots)] if slots is not None else QS[i % 2])
            tt = i % 4
            sl = i % 2
            qr = res("qrot%d" % sl)
            rope(qrot[:, sl, 0:H * 64], qs[:, 0:H * 64], H, tabs[:, sl, 0, :], tabs[:, sl, 1, :],
                 qres + [res("tabs%d" % sl)], qr)
            rope(qrot[:, sl, H * 64:2 * H * 64], qs[:, H * 64:2 * H * 64], H, ropet[:, tt, 128:192], ropet[:, tt, 192:256],
                 qres + [res("ropet%d" % tt)], qr)

        s_kv0 = page_acquire(PG_KV)
        s_kv1 = page_acquire(PG_KV + 1)

        def p1_proj(i):
            tt = i % 4
            st_load_rope(i)
            kb = 4 + (i % 2)
            qs, qres = P1Q[i % 4]
            for half, sl in ((0, s_kv0), (1, s_kv1)):
                for kc in range(8):
                    P.op("pe", lambda e, kc=kc, half=half, sl=sl: e.matmul(
                        pb[kb][:, half * 256:(half + 1) * 256], lhsT=hT[:, kc, tt * 128:(tt + 1) * 128],
                        rhs=pages[:, sl, kc * 256:(kc + 1) * 256], start=(kc == 0), stop=(kc == 7)),
                        reads=[res("hT%d" % tt), slot_res[sl]], writes=[BK[kb]])
            P.op("act", lambda e: e.activation(out=qs[:, 0:256], in_=pb[kb][:, 0:256], func=AF.Copy),
                 reads=[BK[kb]], writes=qres)
            P.op("dve", lambda e: e.tensor_copy(out=V[:, i, 0, :], in_=pb[kb][:, 256:320]),
                 reads=[BK[kb]], writes=[res("V")])
            P.op("dve", lambda e: e.tensor_copy(out=V[:, i, 5, :], in_=pb[kb][:, 320:384]),
                 reads=[BK[kb]], writes=[res("V")])
            P.op("dve", lambda e: e.tensor_copy(
                out=V[:, i, 2:4, :], in_=pb[kb][:, 384:512].rearrange("p (a d) -> p a d", d=64)),
                reads=[BK[kb]], writes=[res("V")])

        def p1_kT(i):
            sl = i % 2
            kbk = 6 + sl
            for c in range(2):
                P.op("pe", lambda e, c=c: e.matmul(pb[kbk][:, c * 128:(c + 1) * 128],
                                                    lhsT=qrot[:, sl, c * 128:(c + 1) * 128], rhs=identb[:],
                                                    start=True, stop=True),
                     reads=[res("qrot%d" % sl), res("identb")], writes=[BK[kbk]])
            P.op("act", lambda e: e.activation(
                out=kT[:, :, i * 128:(i + 1) * 128], in_=pb[kbk][:, 0:256].rearrange("p (c t) -> p c t", t=128),
                func=AF.Copy), reads=[BK[kbk]], writes=[res("kT")])

        pipeline(list(range(NT)), [
            st_load_x, st_ss, st_xn, lambda i: st_T(i, [(0, 1), (2, 3)][i % 2]), p1_proj,
            lambda i: st_headnorm(i, 2, 2, "a", P1Q), lambda i: st_headnorm(i, 2, 2, "b", P1Q),
            lambda i: st_rope(i, 2, P1Q), p1_kT])
        page_load(s_kv0)
        page_load(s_kv1)

        preloaded = {"x": False}
        blocks_ = list(DBG_BLOCKS(NB))
        for bi_, j in enumerate(blocks_):
            t0 = 4 * j
            sq_slots = [page_acquire(PG_Q + k) for k in range(4)]
            if not preloaded["x"]:
                for tt in range(4):
                    st_load_x(t0 + tt)
                    st_load_rope(t0 + tt)
            preloaded["x"] = False

            def ab_proj(i, sq_slots=sq_slots):
                tt = i % 4
                qb_ = (2, 3) if tt % 2 == 0 else (4, 5)
                qs, qres = QS[i % 2]
                for k in range(4):
                    b = qb_[k // 2]
                    for kc in range(8):
                        P.op("pe", lambda e, k=k, kc=kc, b=b: e.matmul(
                            pb[b][:, (k % 2) * 256:(k % 2 + 1) * 256], lhsT=hT[:, kc, tt * 128:(tt + 1) * 128],
                            rhs=pages[:, sq_slots[k], kc * 256:(kc + 1) * 256], start=(kc == 0), stop=(kc == 7)),
                            reads=[res("hT%d" % tt), slot_res[sq_slots[k]]], writes=[BK[b]])
                P.op("act", lambda e: e.activation(out=qs[:, 0:1024].rearrange("p (a c) -> p a c", a=2),
                                                   in_=pball[:, qb_[0]:qb_[0] + 2, :], func=AF.Copy),
                     reads=[BK[qb_[0]], BK[qb_[1]]], writes=qres)

            def ab_qT(i):
                tt = i % 4
                sl = i % 2
                transposes_to(lambda lo, hi: qT[:, lo:hi, tt * 128:(tt + 1) * 128],
                              lambda c: qrot[:, sl, c * 128:(c + 1) * 128], res("qrot%d" % sl), 8, (6, 7), res("qT"), "q")

            pipeline([t0 + tt for tt in range(4)], [
                st_ss, st_xn, lambda i: st_T(i, (0, 1)), ab_proj,
                lambda i: st_headnorm(i, 8, 0), lambda i: st_rope(i, 8), ab_qT])
            for s_ in sq_slots:
                page_load(s_)

            d_items = [(g, half, nh) for g in range(4) for half in range(2) for nh in range(2)]
            items = []
            for g in range(4):
                for kt in range(NT):
                    items.append(("C", g, kt))
                    cnt_c = len([1 for it in items if it[0] == "C"])
                    if d_items and ((cnt_c % 8 == 0 and cnt_c < 4 * NT) or cnt_c == 4 * NT - 4):
                        items.append(("D",) + d_items.pop(0))
            while d_items:
                items.append(("D",) + d_items.pop(0))
            NI = len(items)
            SBK = ((0, 1), (2, 3))
            pending_stats = []
            cur_iter = {"u": 0}
            PTW4 = PTW
            dcount = {"n": 0}
            dslot = {}

            def xbank(g, half):
                return 4 + (2 * g + half) % 3

            def wvalid(n):
                qi = t0 + n
                return [m for m in (qi - 1, qi, qi + 1) if 0 <= m < NT]

            def flush_stats(b1, now=None, n_max=8):
                colbase = 256
                while pending_stats and n_max > 0 and (now is None or pending_stats[0][0] <= now):
                    pending_stats.pop(0)[1](b1, colbase)
                    colbase += 4
                    n_max -= 1

            def emit_S(u):
                it = items[u]
                b0, b1 = SBK[u % 2]
                if it[0] == "C":
                    _, g, kt = it
                    for half in range(2):
                        b = SBK[u % 2][half]
                        P.op("pe", lambda e, g=g, half=half, kt=kt, b=b: e.matmul(
                            pb[b], lhsT=kT[half * 64:(half + 1) * 64, 0, kt * 128:(kt + 1) * 128],
                            rhs=qT[half * 64:(half + 1) * 64, g, :], start=True, stop=True),
                            reads=[res("kT"), res("qT")], writes=[BK[b]])
                else:
                    _, g, half, nh = it
                    ps2 = pball[:, b0:b0 + 2, :].rearrange("p a c -> p (a c)")
                    for nl in range(2):
                        n = 2 * nh + nl
                        qi = t0 + n
                        for m in wvalid(n):
                            mi = m - (qi - 1)
                            col = nl * 384 + mi * 128
                            P.op("pe", lambda e, m=m, col=col, g=g, half=half, n=n, ps2=ps2, mi=mi: e.matmul(
                                ps2[:, col:col + 128], lhsT=kT[half * 64:(half + 1) * 64, 1, m * 128:(m + 1) * 128],
                                rhs=qT[half * 64:(half + 1) * 64, 4 + g, n * 128:(n + 1) * 128], start=True, stop=(mi == 1)),
                                reads=[res("kT"), res("qT")], writes=[BK[b0 + col // 512]])
                            if mi != 1:
                                P.op("pe", lambda e, col=col, ps2=ps2, mi=mi: e.matmul(
                                    ps2[:, col:col + 128], lhsT=identb[:], rhs=maskb[:, mi * 128:(mi + 1) * 128],
                                    start=False, stop=True),
                                    reads=[res("identb"), res("maskb")], writes=[BK[b0 + col // 512]])

            def emit_E(u):
                it = items[u]
                b0, b1 = SBK[u % 2]
                if it[0] == "C":
                    pt = u % 3
                    P.op("act", lambda e, b0=b0, pt=pt: e.activation(
                        out=PT[:, pt, :].rearrange("p (a c) -> p a c", a=2), in_=pball[:, b0:b0 + 2, :], func=AF.Exp, scale=0.125),
                        reads=[BK[b0], BK[b1]], writes=[res("PT%d" % pt)])
                else:
                    pw = 0
                    dcount["n"] += 1
                    dslot[u] = pw
                    pr = res("PTW%d" % pw)
                    ps2 = pball[:, b0:b0 + 2, :].rearrange("p a c -> p (a c)")
                    P.op("act", lambda e, pw=pw, ps2=ps2: e.activation(out=PTW4[:, pw, :], in_=ps2[:, 0:768], func=AF.Exp, scale=0.125),
                         reads=[BK[b0], BK[b1]], writes=[pr])

            def chunk_finish(c, sl, first):
                qsl = c
                gate = 0
                onr, sqr = res("onrm%d" % sl), res("sqb%d" % qsl)
                P.op("pool", lambda e: e.tensor_tensor(out=sqb[:, qsl, :], in0=onrm[:, sl, :], in1=onrm[:, sl, :],
                                                       op=ALU.mult), reads=[onr], writes=[sqr])
                P.op("dve", lambda e: e.tensor_scalar(out=attnT[:, c, :], in0=onrm[:, sl, :], scalar1=gcol[:, c:c + 1],
                                                      scalar2=None, op0=ALU.mult),
                     reads=[onr, res("gcol")], writes=[res("attnT%d" % c)])
                gbase = 16 + 4 * (c // 4)

                def emit_stats(b1, colbase):
                    for tt in range(4):
                        P.op("pe", lambda e, tt=tt: e.matmul(
                            pb[b1][:, colbase + tt:colbase + tt + 1], lhsT=sqb[:, qsl, tt * 128:(tt + 1) * 128], rhs=ones_c[:, 0:1],
                            start=True, stop=True, skip_group_check=True), reads=[sqr, res("ones_c")], writes=[BK[b1]])
                    if first:
                        P.op("dve", lambda e: e.tensor_copy(out=stat[:, gbase:gbase + 4], in_=pb[b1][:, colbase:colbase + 4]),
                             reads=[BK[b1]], writes=[res("gss")])
                    else:
                        P.op("dve", lambda e: e.tensor_tensor(out=stat[:, gbase:gbase + 4], in0=stat[:, gbase:gbase + 4],
                                                              in1=pb[b1][:, colbase:colbase + 4], op=ALU.add),
                             reads=[BK[b1], res("gss")], writes=[res("gss")])
                pending_stats.append((cur_iter["u"] + gate, emit_stats))

            def epi_half(c, bank, o_lo, den_lo, sink_head, urgent=False):
                rs = c % 2
                sl = c % 2 if c < 4 else 2
                rr, onr = res("rec%d" % rs), res("onrm%d" % sl)
                if urgent and not _os.environ.get('KNOURGENT'):
                    P.op("act", lambda e: e.activation(out=rec[:, rs, :], in_=pb[bank], func=AF.Copy), reads=[BK[bank]], writes=[rr])
                else:
                    P.op("dve", lambda e: e.tensor_copy(out=rec[:, rs, :], in_=pb[bank]), reads=[BK[bank]], writes=[rr])
                if sink_head is None:
                    P.op("dve", lambda e: e.reciprocal(out=onrm[o_lo:o_lo + 64, sl, :], in_=rec[den_lo:den_lo + 64, rs, :]),
                         reads=[rr], writes=[onr])
                else:
                    P.op("dve", lambda e: e.tensor_scalar(out=onrm[o_lo:o_lo + 64, sl, :], in0=rec[den_lo:den_lo + 64, rs, :],
                                                          scalar1=es[o_lo:o_lo + 64, sink_head:sink_head + 1], scalar2=None,
                                                          op0=ALU.add), reads=[rr, res("es")], writes=[onr])
                    P.op("dve", lambda e: e.reciprocal(out=onrm[o_lo:o_lo + 64, sl, :], in_=onrm[o_lo:o_lo + 64, sl, :]),
                         reads=[onr], writes=[onr])
                P.op("dve", lambda e: e.tensor_tensor(out=onrm[o_lo:o_lo + 64, sl, :], in0=rec[o_lo:o_lo + 64, rs, :],
                                                      in1=onrm[o_lo:o_lo + 64, sl, :], op=ALU.mult), reads=[rr, onr], writes=[onr])

            def emit_PV(u):
                it = items[u]
                if it[0] == "C":
                    _, g, kt = it
                    pt = u % 3
                    ptr = res("PT%d" % pt)
                    for half in range(2):
                        xb_ = xbank(g, half)
                        vb = (0, 2) if half == 0 else (4, 6)
                        P.op("pe", lambda e, kt=kt, pt=pt, xb_=xb_, vb=vb, half=half: e.matmul(
                            pb[xb_], lhsT=V[:, kt, vb[0]:vb[1], :].rearrange("p a d -> p (a d)"),
                            rhs=PT[:, pt, half * 512:(half + 1) * 512],
                            start=(kt == 0), stop=(kt == NT - 1)), reads=[res("V"), res("Vones"), ptr], writes=[BK[xb_]])
                    if kt == NT - 1:
                        epi_half(g, xbank(g, 0), 0, 64, None, urgent=True)
                        epi_half(g, xbank(g, 1), 64, 0, None, urgent=(g == 3))
                        chunk_finish(g, g % 2, g == 0)
                else:
                    _, g, half, nh = it
                    pw = dslot[u]
                    pr = res("PTW%d" % pw)
                    vb = (1, 3) if half == 0 else (3, 5)
                    for nl in range(2):
                        n = 2 * nh + nl
                        qi = t0 + n
                        ms = wvalid(n)
                        for m in ms:
                            col = nl * 384 + (m - (qi - 1)) * 128
                            P.op("pe", lambda e, m=m, col=col, pw=pw, vb=vb, n=n, f=(m == ms[0]), l=(m == ms[-1]): e.matmul(
                                pb[7][:, n * 128:(n + 1) * 128], lhsT=V[:, m, vb[0]:vb[1], :].rearrange("p a d -> p (a d)"),
                                rhs=PTW4[:, pw, col:col + 128], start=f, stop=l, skip_group_check=True),
                                reads=[res("V"), res("Vones"), pr], writes=[BK[7]])
                    if nh == 1:
                        c = 4 + g
                        if half == 0:
                            epi_half(c, 7, 64, 0, g)
                        else:
                            epi_half(c, 7, 0, 64, 4 + g)
                            chunk_finish(c, 2, g == 0)

            emit_S(0)
            if NI > 1:
                emit_S(1)
            late = []
            for u in range(NI):
                cur_iter["u"] = u
                emit_E(u)
                if u + 2 < NI:
                    emit_S(u + 2)
                while late:
                    emit_PV(late.pop(0))
                emit_PV(u)
            while late:
                emit_PV(late.pop(0))
            while pending_stats:
                flush_stats(1, None)

            rstd_from_ss(stat[:, 16:24], res("gss"), 512, 8)
            if dbga_d is not None:
                P.op("pool", lambda e: e.dma_start(out=dbga_d, in_=attnT[:]), reads=[res("attnT%d" % c) for c in range(8)],
                     dma=True, sem_res=res("dbga"))
            so_slots = [page_acquire(PG_O + k) for k in range(4)]
            MS = [(mixed, res("mixed")), (qtok, res("qtok"))]
            MCOL = [1, 4]
            FCOL = [2, 5]

            def e_proj(tt, so_slots=so_slots):
                base = 4 * (tt % 2)
                for grp in range(2):
                    for k in range(4):
                        b = base + 2 * grp + k // 2
                        for c in range(4):
                            cc = 4 * grp + c
                            P.op("pe", lambda e, b=b, k=k, cc=cc, c=c: e.matmul(
                                pb[b][:, (k % 2) * 256:(k % 2 + 1) * 256], lhsT=attnT[:, cc, tt * 128:(tt + 1) * 128],
                                rhs=pages[:, so_slots[k], cc * 256:(cc + 1) * 256], start=(c == 0), stop=(c == 3)),
                                reads=[res("attnT%d" % cc), slot_res[so_slots[k]]], writes=[BK[b]])

            def e_mix(tt):
                base = 4 * (tt % 2)
                mx, mr = MS[tt % 2]
                mx3 = mx[:].rearrange("p (a c) -> p a c", a=2)
                P.op("dve", lambda e: e.tensor_scalar(
                    out=mx3, in0=pball[:, base:base + 2, :], scalar1=stat[:, 16 + tt:17 + tt],
                    scalar2=None, op0=ALU.mult), reads=[BK[base], BK[base + 1], res("gss")], writes=[mr])
                P.op("dve", lambda e: e.scalar_tensor_tensor(
                    out=mx3, in0=pball[:, base + 2:base + 4, :], scalar=stat[:, 20 + tt:21 + tt],
                    in1=mx3, op0=ALU.mult, op1=ALU.add),
                    reads=[BK[base + 2], BK[base + 3], res("gss"), mr], writes=[mr])

            def e_ss1(tt):
                mx, mr = MS[tt % 2]
                col = MCOL[tt % 2]
                ssr = res("ss_col%d" % col)
                P.op("act", lambda e: e.activation(out=junk[:], in_=mx[:], func=AF.Square, accum_out=stat[:, col:col + 1]),
                     reads=[mr], writes=[res("PT0"), ssr])
                rstd_from_ss(stat[:, col:col + 1], ssr, D)

            def e_res(tt):
                mx, mr = MS[tt % 2]
                col = MCOL[tt % 2]
                xr = res("xblk%d" % tt)
                P.op("dve", lambda e: e.scalar_tensor_tensor(out=mx[:], in0=mx[:], scalar=stat[:, col:col + 1],
                                                             in1=gains[:, 1, :], op0=ALU.mult, op1=ALU.mult),
                     reads=[mr, res("ss_col%d" % col), res("gains")], writes=[mr])
                P.op("pool", lambda e: e.tensor_tensor(out=xblk[:, tt, :], in0=xblk[:, tt, :], in1=mx[:],
                                                       op=ALU.add), reads=[xr, mr], writes=[xr])
                if dbg_d is not None:
                    P.op("sp", lambda e, i=t0 + tt: e.dma_start(out=dbg_d[i * 128:(i + 1) * 128, :], in_=xblk[:, tt, :]),
                         reads=[xr], dma=True, sem_res=res("dbg%d" % tt))

            def e_ss2(tt):
                xr = res("xblk%d" % tt)
                col = FCOL[tt % 2]
                ssr = res("ss_col%d" % col)
                P.op("act", lambda e: e.activation(out=junk[:], in_=xblk[:, tt, :], func=AF.Square,
                                                   accum_out=stat[:, col:col + 1]),
                     reads=[xr], writes=[res("PT0"), ssr])
                rstd_from_ss(stat[:, col:col + 1], ssr, D)

            def e_h2(tt):
                xr = res("xblk%d" % tt)
                col = FCOL[tt % 2]
                sl = tt % 2
                P.op("dve", lambda e: e.scalar_tensor_tensor(out=xn[:, sl, :], in0=xblk[:, tt, :], scalar=stat[:, col:col + 1],
                                                             in1=gains[:, 2, :], op0=ALU.mult, op1=ALU.mult),
                     reads=[xr, res("ss_col%d" % col), res("gains")], writes=[res("xn%d" % sl)])

            def e_h2T(tt):
                sl = tt % 2
                tb = [(4, 5), (6, 7)][tt % 2]
                transposes_to(lambda lo, hi: hT[:, lo:hi, tt * 128:(tt + 1) * 128],
                              lambda c: xn[:, sl, c * 128:(c + 1) * 128], res("xn%d" % sl), 8, tb, res("hT%d" % tt), "f")

            pipeline([0, 1, 2, 3], [e_proj, e_mix, e_ss1, e_res, e_ss2, e_h2, e_h2T])
            for s_ in so_slots:
                page_load(s_)

            hT_all = [res("hT%d" % tt) for tt in range(4)]
            for k in range(11):
                sg_ = page_acquire(PG_G + k)
                su_ = page_acquire(PG_U + k)
                for fl in range(2):
                    fc = 2 * k + fl
                    gbk, ubk = (0, 1) if fc % 2 == 0 else (2, 3)
                    for kc in range(8):
                        P.op("pe", lambda e, kc=kc, fl=fl, sg_=sg_, gbk=gbk: e.matmul(
                            pb[gbk], lhsT=pages[:, sg_, kc * 256 + fl * 128:kc * 256 + (fl + 1) * 128],
                            rhs=hT[:, kc, :], start=(kc == 0), stop=(kc == 7)),
                            reads=hT_all + [slot_res[sg_]], writes=[BK[gbk]])
                    for kc in range(8):
                        P.op("pe", lambda e, kc=kc, fl=fl, su_=su_, ubk=ubk: e.matmul(
                            pb[ubk], lhsT=pages[:, su_, kc * 256 + fl * 128:kc * 256 + (fl + 1) * 128],
                            rhs=hT[:, kc, :], start=(kc == 0), stop=(kc == 7)),
                            reads=hT_all + [slot_res[su_]], writes=[BK[ubk]])
                    sgr = res("sg%d" % (fc % 2))
                    P.op("act", lambda e, fc=fc, gbk=gbk: e.activation(out=sg[:, fc % 2, :], in_=pb[gbk], func=AF.Silu),
                         reads=[BK[gbk]], writes=[sgr])
                    P.op("dve", lambda e, fc=fc, ubk=ubk: e.tensor_tensor(out=aT[:, fc, :], in0=sg[:, fc % 2, :],
                                                                          in1=pb[ubk], op=ALU.mult),
                         reads=[sgr, BK[ubk]], writes=[res("aT")])
                page_load(sg_)
                page_load(su_)
            ssf = res("ss_f")
            P.op("act", lambda e: e.activation(out=stat[:, 60:61], in_=epsc[:, 0:1], func=AF.Ln),
                 reads=[res("epsc")], writes=[res("tblwarm")])
            for cb in range(2):
                for grp in range(6):
                    sd_ = page_acquire(PG_D + cb * 6 + grp)
                    nfc = min(4, NFC - 4 * grp)
                    for tt in range(4):
                        bk = 4 * (1 - cb) + tt
                        for f4 in range(nfc):
                            fc = 4 * grp + f4
                            P.op("pe", lambda e, tt=tt, f4=f4, fc=fc, sd_=sd_, bk=bk: e.matmul(
                                pb[bk], lhsT=aT[:, fc, tt * 128:(tt + 1) * 128],
                                rhs=pages[:, sd_, f4 * 512:(f4 + 1) * 512], start=(fc == 0), stop=(fc == NFC - 1)),
                                reads=[res("aT"), slot_res[sd_]], writes=[BK[bk]])
                    page_load(sd_)
                for tt in range(4):
                    bk = 4 * (1 - cb) + tt
                    P.op("act", lambda e, tt=tt, bk=bk, cb=cb: e.activation(
                        out=junk[:, 0:512], in_=pb[bk], func=AF.Square, accum_out=stat[:, 32 + 4 * cb + tt:33 + 4 * cb + tt]),
                        reads=[BK[bk]], writes=[res("PT0"), ssf])
            P.op("dve", lambda e: e.tensor_tensor(out=stat[:, 32:36], in0=stat[:, 32:36], in1=stat[:, 36:40],
                                                  op=ALU.add), reads=[ssf], writes=[ssf])
            rstd_from_ss(stat[:, 32:36], ssf, D, 4)
            for tt in range(4):
                mx, mr = MS[tt % 2]
                for cb in range(2):
                    bk = 4 * (1 - cb) + tt
                    P.op("dve", lambda e, tt=tt, cb=cb, bk=bk, mx=mx: e.scalar_tensor_tensor(
                        out=mx[:, cb * 512:(cb + 1) * 512], in0=pb[bk], scalar=stat[:, 32 + tt:33 + tt],
                        in1=gains[:, 3, cb * 512:(cb + 1) * 512], op0=ALU.mult, op1=ALU.mult),
                        reads=[BK[bk], ssf, res("gains")], writes=[mr])
                xr = res("xblk%d" % tt)
                P.op("pool" if tt % 2 == 0 else "dve", lambda e, tt=tt, mx=mx: e.tensor_tensor(
                    out=xblk[:, tt, :], in0=xblk[:, tt, :], in1=mx[:], op=ALU.add), reads=[xr, mr], writes=[xr])
                i = t0 + tt
                P.op("sp", lambda e, tt=tt, i=i: e.dma_start(out=out_d[i * 128:(i + 1) * 128, :], in_=xblk[:, tt, :]),
                     reads=[xr], dma=True, sem_res=res("ost%d" % tt))
                if bi_ + 1 < len(blocks_) and tt >= 1:
                    st_load_x(4 * blocks_[bi_ + 1] + tt - 1)
                    st_load_rope(4 * blocks_[bi_ + 1] + tt - 1)
            if bi_ + 1 < len(blocks_):
                st_load_x(4 * blocks_[bi_ + 1] + 3)
                st_load_rope(4 * blocks_[bi_ + 1] + 3)
                preloaded["x"] = True
        stats = P.emit()
    return nc, stats


def _rope_table(S):
    t = np.arange(S)
    row = (t // 64).astype(np.float32)
    col = (t % 64).astype(np.float32)
    f_ax = (np.float32(THETA) ** (-(np.arange(16, dtype=np.float32) / np.float32(16)))).astype(np.float32)
    ang_ax = np.concatenate([row[:, None] * f_ax[None, :], col[:, None] * f_ax[None, :]], axis=-1).astype(np.float32)
    f_1d = (np.float32(THETA) ** (-(np.arange(32, dtype=np.float32) / np.float32(32)))).astype(np.float32)
    ang_1d = (t.astype(np.float32)[:, None] * f_1d[None, :]).astype(np.float32)
    tabs = []
    for ang in (ang_ax, ang_1d):
        c = np.cos(ang).astype(np.float32)
        s = np.sin(ang).astype(np.float32)
        tabs.append(np.concatenate([c, c], axis=-1))
        tabs.append(np.concatenate([-s, s], axis=-1))
    return np.ascontiguousarray(np.concatenate(tabs, axis=-1), dtype=np.float32)


def _prep_shared(inp, S):
    w_in = np.asarray(inp["w_in"], np.float32)[0]
    qA, kA, vA = w_in[:, 0:512], w_in[:, 512:640], w_in[:, 640:768]
    qB, kB, vB = w_in[:, 768:1280], w_in[:, 1280:1408], w_in[:, 1408:1536]

    def pair(q):
        return np.concatenate([np.concatenate([q[:, g * 64:(g + 1) * 64], q[:, (4 + g) * 64:(5 + g) * 64]], 1)
                               for g in range(4)], 1)

    w_kv = np.concatenate([kA, kB, vA, vB], 1)
    w_q = np.concatenate([pair(qA), pair(qB)], 1)
    w_out = np.asarray(inp["w_out"], np.float32)[0]
    rows = []
    for c in range(4):
        rows += list(range(c * 64, (c + 1) * 64)) + list(range((4 + c) * 64, (5 + c) * 64))
    for c in range(4):
        rows += list(range(512 + (4 + c) * 64, 512 + (5 + c) * 64)) + list(range(512 + c * 64, 512 + (c + 1) * 64))
    w_o = w_out[np.array(rows)]
    ga = np.asarray(inp["group_norm_a"], np.float32)[0]
    gb = np.asarray(inp["group_norm_b"], np.float32)[0]
    gcat = np.concatenate([ga, gb])
    gcol = gcat[np.array(rows)].reshape(8, 128).T
    qn = np.asarray(inp["q_norm_a"], np.float32)[0]
    kn = np.asarray(inp["k_norm_a"], np.float32)[0]
    sw_ = lambda g: np.concatenate([g[32:], g[:32]])
    qk = np.stack([qn, sw_(qn), kn, sw_(kn)], 0)
    gains = np.stack([np.asarray(inp[k], np.float32)[0] for k in
                      ("norm_mix_pre", "norm_mix_post", "norm_ffn_pre", "norm_ffn_post")], 0)
    kk = np.arange(128)[:, None]
    qq = np.arange(128)[None, :]
    valid = np.concatenate([(qq <= kk), np.ones((128, 128), bool), (kk <= qq)], 1)
    mask = np.where(valid, 0.0, -30000.0).astype(np.float32)
    c = np.ascontiguousarray
    return {
        "w_kv": c(w_kv), "w_q": c(w_q), "w_o": c(w_o),
        "w_g": c(np.asarray(inp["w_gate"], np.float32)[0]), "w_u": c(np.asarray(inp["w_up"], np.float32)[0]),
        "w_d": c(np.asarray(inp["w_down"], np.float32)[0]),
        "gains": c(gains), "qk_gain": c(qk), "sink": c(np.asarray(inp["sink_b"], np.float32).reshape(1, 8)),
        "gcol": c(gcol.astype(np.float32)), "rope": _rope_table(S), "ident": np.eye(128, dtype=np.float32),
        "mask": c(mask),
    }


_CACHE = {}


def kernel(**inputs):
    x = np.asarray(inputs["x"], np.float32)
    B, S, _ = x.shape
    shared = _prep_shared(inputs, S)
    if S not in _CACHE:
        _CACHE[S] = build_program(S)[0]
    nc = _CACHE[S]
    in_maps = []
    for b in range(B):
        m = dict(shared)
        m["x"] = np.ascontiguousarray(x[b])
        in_maps.append(m)
    res = run_bass_kernel_spmd(nc, in_maps, core_ids=list(range(B)))
    return np.stack([np.asarray(r["out"], np.float32) for r in res.results], 0)
```

```python
import numpy as np
from contextlib import ExitStack
import concourse.bass as bass
import concourse.mybir as mybir
from concourse.bass_utils import run_bass_kernel_spmd

F32 = mybir.dt.float32
BF16 = mybir.dt.bfloat16
AF = mybir.ActivationFunctionType
ALU = mybir.AluOpType
AX = mybir.AxisListType

D = 1024
DFF = 2816
NFC = DFF // 128
EPS = 1e-6
THETA = 10000.0

COMPUTE = ("pe", "act", "dve", "pool")
ALLQ = ("pe", "act", "dve", "pool", "sp")


class Res:
    __slots__ = ("name", "last_w", "rd_eng", "rd_dma", "sem", "dma_cnt", "excl", "last_real")

    def __init__(self, name, excl=False):
        self.name = name
        self.last_w = None
        self.rd_eng = {}
        self.rd_dma = []
        self.sem = None
        self.dma_cnt = 0
        self.excl = excl
        self.last_real = True


class Op:
    __slots__ = ("eng", "fn", "deps", "dma", "signal", "sig_val", "sem_res")

    def __init__(self, eng, fn, dma, sem_res):
        self.eng = eng
        self.fn = fn
        self.deps = []
        self.dma = dma
        self.signal = dma
        self.sig_val = 0
        self.sem_res = sem_res


class Prog:
    def __init__(self, nc, stack):
        self.nc = nc
        self.stack = stack
        self.ops = []
        self.dma_res = []

    def op(self, eng, fn, reads=(), writes=(), dma=False, sem_res=None):
        o = Op(eng, fn, dma, sem_res)
        deps = {}
        xreads = [r for r in reads if r.excl]
        reads = [r for r in reads if not r.excl]
        for r in xreads:
            p = r.last_w
            if p is not None and not (p.eng == eng and not r.last_real and not p.dma):
                deps[id(p)] = p
        for r in reads:
            if r.last_w is not None:
                deps[id(r.last_w)] = r.last_w
        for w in writes:
            if w.last_w is not None:
                deps[id(w.last_w)] = w.last_w
            for p in w.rd_eng.values():
                deps[id(p)] = p
            for p in w.rd_dma:
                deps[id(p)] = p
        for p in deps.values():
            if p.eng == "pe" and eng == "pe" and not p.dma and not dma:
                continue
            p.signal = True
            o.deps.append(p)
        for r in reads:
            if dma:
                r.rd_dma.append(o)
            else:
                r.rd_eng[eng] = o
        for w in writes:
            w.last_w = o
            w.last_real = True
            w.rd_eng = {}
            w.rd_dma = []
        for r in xreads:
            if r.last_w is not o:
                r.last_w = o
                r.last_real = False
        if dma:
            if sem_res.sem is None:
                sem_res.sem = self.stack.enter_context(self.nc.semaphore("d_" + sem_res.name))
                self.dma_res.append(sem_res)
            sem_res.dma_cnt += 1
            o.sig_val = 16 * sem_res.dma_cnt
        self.ops.append(o)
        return o

    def emit(self):
        nc = self.nc
        esem = {e: self.stack.enter_context(nc.semaphore("e_" + e)) for e in COMPUTE}
        cnt = {e: 0 for e in COMPUTE}
        if _os.environ.get("KALLSIG"):
            for o in self.ops:
                if o.eng in _os.environ["KALLSIG"].split(","):
                    o.signal = True
        for o in self.ops:
            if not o.dma and o.signal:
                cnt[o.eng] += 1
                o.sig_val = cnt[o.eng]
        per = {e: [] for e in ALLQ}
        for o in self.ops:
            per[o.eng].append(o)
        finals = [(r.sem, 16 * r.dma_cnt) for r in self.dma_res]

        def run(eng_name, eng):
            seen = {}
            for o in per[eng_name]:
                for p in o.deps:
                    if p.dma:
                        s, v = p.sem_res.sem, p.sig_val
                    else:
                        s, v = esem[p.eng], p.sig_val
                    k = id(s)
                    if seen.get(k, 0) < v:
                        eng.wait_ge(s, v)
                        seen[k] = v
                ins = o.fn(eng)
                if o.dma:
                    ins.then_inc(o.sem_res.sem, 16)
                elif o.signal:
                    ins.then_inc(esem[o.eng], 1)
            if eng_name == "sp":
                for s, v in finals:
                    eng.wait_ge(s, v)

        with nc.Block() as block:
            @block.sync
            def _(e):
                run("sp", e)

            @block.tensor
            def _(e):
                run("pe", e)

            @block.scalar
            def _(e):
                run("act", e)

            @block.vector
            def _(e):
                run("dve", e)

            @block.gpsimd
            def _(e):
                run("pool", e)
        return {e: len(per[e]) for e in ALLQ}, cnt


PG_KV = 0
PG_Q = 2
PG_O = 6
PG_G = 10
PG_U = 21
PG_D = 32
NPAGES = 44
NSLOT = 8


import os as _os


def DBG_BLOCKS(NB):
    v = _os.environ.get('KBLOCKS')
    return range(NB) if v is None else [int(t) for t in v.split(',')]


def build_program(S):
    NT = S // 128
    NB = S // 512
    nc = bass.Bass("TRN2", target_bir_lowering=False)

    def din(name, shape):
        return nc.dram_tensor(name, list(shape), F32, kind="ExternalInput").ap()

    x_d = din("x", [S, D])
    wkv_d = din("w_kv", [D, 512])
    wq_d = din("w_q", [D, 1024])
    wo_d = din("w_o", [D, 1024])
    wg_d = din("w_g", [D, DFF])
    wu_d = din("w_u", [D, DFF])
    wd_d = din("w_d", [DFF, D])
    gains_d = din("gains", [4, D])
    qk_d = din("qk_gain", [4, 64])
    sink_d = din("sink", [1, 8])
    gcol_d = din("gcol", [128, 8])
    rope_d = din("rope", [S, 256])
    ident_d = din("ident", [128, 128])
    mask_d = din("mask", [128, 384])
    out_d = nc.dram_tensor("out", [S, D], F32, kind="ExternalOutput").ap()
    dbg_d = nc.dram_tensor("dbg", [S, D], F32, kind="ExternalOutput").ap() if _os.environ.get("KDBG") else None
    scr = nc.dram_tensor("wscr", [NPAGES, 128, 2048], BF16).ap()
    dbga_d = nc.dram_tensor("dbga", [128, 8, 512], F32, kind="ExternalOutput").ap() if _os.environ.get("KDBGA") else None

    with ExitStack() as st:
        def sb(name, shape, dt):
            return st.enter_context(nc.sbuf_tensor("s_" + name, list(shape), dt))

        pages = sb("pages", [128, NSLOT, 2048], BF16)
        kT = sb("kT", [128, 2, S], BF16)
        V = sb("V", [128, NT, 6, 64], BF16)
        gains = sb("gains", [128, 4, D], F32)
        qkg = sb("qkg", [128, 4, 64], F32)
        es = sb("es", [128, 8], F32)
        gcol = sb("gcol", [128, 8], F32)
        identb = sb("identb", [128, 128], BF16)
        maskb = sb("maskb", [128, 384], BF16)
        ones_c = sb("ones_c", [128, 8], BF16)
        epsc = sb("epsc", [128, 8], F32)
        xblk = sb("xblk", [128, 4, D], F32)
        ropet = sb("ropet", [128, 4, 256], F32)
        tabs = sb("tabs", [128, 2, 2, 64], F32)
        xn = sb("xn", [128, 2, D], BF16)
        hT = sb("hT", [128, 8, 512], BF16)
        qtok = sb("qtok", [128, D], F32)
        tq = sb("tq", [128, 512], F32)
        sw = sb("sw", [128, 512], F32)
        qrot = sb("qrot", [128, 2, D], BF16)
        qT = sb("qT", [128, 8, 512], BF16)
        PT = sb("PT", [128, 3, 1024], BF16)
        PTW = sb("PTW", [128, 1, 768], BF16)
        junk = PT[:, 0, :]
        attnT = sb("attnT", [128, 8, 512], BF16)
        ro = sb("ro", [128, 5, 512], F32)
        rec = ro[:, 0:2, :]
        onrm = ro[:, 2:5, :]
        sqb = sb("sqb", [128, 8, 512], BF16)
        mixed = sb("mixed", [128, D], F32)
        aT = sb("aT", [128, NFC, 512], BF16)
        sg = sb("sg", [128, 2, 512], F32)
        stat = sb("stat", [128, 64], F32)

        pball = st.enter_context(nc.psum_tensor("pball", [128, 8, 512], F32))
        pb = [pball[:, i, :] for i in range(8)]
        pp = [st.enter_context(nc.psum_tensor("pp%d" % i, [128, 1024], F32)) for i in range(0)]

        P = Prog(nc, st)
        R = {}

        def res(name, excl=False):
            if name not in R:
                R[name] = Res(name, excl)
            return R[name]

        BK = [res("bank%d" % i, excl=True) for i in range(8)]
        scr_res = [res("scr%d" % i) for i in range(NPAGES)]

        def cast_page(pg, src_ap):
            P.op("pool", lambda e: e.dma_start(out=scr[pg].rearrange("p (a f) -> p a f", a=src_ap.shape[1]),
                                               in_=src_ap),
                 writes=[scr_res[pg]], dma=True, sem_res=scr_res[pg])

        wkv_v = wkv_d.rearrange("(kc p) f -> p kc f", p=128)
        wq_v = wq_d.rearrange("(kc p) f -> p kc f", p=128)
        wo_v = wo_d.rearrange("(kc p) f -> p kc f", p=128)
        wg_v = wg_d.rearrange("(kc p) f -> p kc f", p=128)
        wu_v = wu_d.rearrange("(kc p) f -> p kc f", p=128)
        wd_v = wd_d.rearrange("(fc p) d -> p fc d", p=128)
        for k in range(2):
            cast_page(PG_KV + k, wkv_v[:, :, k * 256:(k + 1) * 256])
        cres = res("consts")
        P.op("sp", lambda e: e.dma_start(out=gains[:].rearrange("p a d -> p (a d)"),
                                         in_=gains_d.rearrange("a d -> (a d)").partition_broadcast(128)),
             writes=[res("gains")], dma=True, sem_res=res("gains"))
        P.op("sp", lambda e: e.dma_start(out=qkg[:].rearrange("p a d -> p (a d)"),
                                         in_=qk_d.rearrange("a d -> (a d)").partition_broadcast(128)),
             writes=[res("qkg")], dma=True, sem_res=res("qkg"))
        P.op("sp", lambda e: e.dma_start(out=es[:], in_=sink_d.rearrange("a d -> (a d)").partition_broadcast(128)),
             writes=[res("es")], dma=True, sem_res=res("es"))
        P.op("sp", lambda e: e.dma_start(out=gcol[:], in_=gcol_d), writes=[res("gcol")], dma=True, sem_res=res("gcol"))
        P.op("pool", lambda e: e.dma_start(out=identb[:], in_=ident_d), writes=[res("identb")], dma=True,
             sem_res=res("identb"))
        P.op("pool", lambda e: e.dma_start(out=maskb[:], in_=mask_d), writes=[res("maskb")], dma=True,
             sem_res=res("maskb"))
        for k in range(4):
            cast_page(PG_Q + k, wq_v[:, :, k * 256:(k + 1) * 256])
        for k in range(4):
            cast_page(PG_O + k, wo_v[:, :, k * 256:(k + 1) * 256])
        for k in range(11):
            cast_page(PG_G + k, wg_v[:, :, k * 256:(k + 1) * 256])
            cast_page(PG_U + k, wu_v[:, :, k * 256:(k + 1) * 256])
        for cb in range(2):
            for grp in range(6):
                nfc = min(4, NFC - 4 * grp)
                pg = PG_D + cb * 6 + grp
                P.op("pool", lambda e, pg=pg, grp=grp, nfc=nfc, cb=cb: e.dma_start(
                    out=scr[pg].rearrange("p (a f) -> p a f", a=4)[:, 0:nfc, :],
                    in_=wd_v[:, 4 * grp:4 * grp + nfc, cb * 512:(cb + 1) * 512]),
                    writes=[scr_res[pg]], dma=True, sem_res=scr_res[pg])
        P.op("act", lambda e: e.activation(out=es[:], in_=es[:], func=AF.Exp), reads=[res("es")], writes=[res("es")])
        P.op("pool", lambda e: e.memset(V[:, :, 1, :], 1.0), writes=[res("Vones")])
        P.op("pool", lambda e: e.memset(V[:, :, 4, :], 1.0), writes=[res("Vones")])
        P.op("pool", lambda e: e.memset(ones_c[:], 1.0), writes=[res("ones_c")])
        P.op("pool", lambda e: e.memset(epsc[:], EPS), writes=[res("epsc")])

        sched = [PG_KV, PG_KV + 1]
        for j in DBG_BLOCKS(NB):
            sched += [PG_Q + k for k in range(4)]
            sched += [PG_O + k for k in range(4)]
            for k in range(11):
                sched += [PG_G + k, PG_U + k]
            sched += [PG_D + k for k in range(12)]
        slot_res = [res("slot%d" % s) for s in range(NSLOT)]
        pstate = {"next": 0, "fifo": []}

        def page_load(slot):
            i = pstate["next"]
            if i >= len(sched):
                return
            pstate["next"] = i + 1
            pg = sched[i]
            w = 1024 if pg in (PG_D + 5, PG_D + 11) else 2048
            P.op("sp", lambda e, slot=slot, pg=pg, w=w: e.dma_start(out=pages[:, slot, 0:w], in_=scr[pg][:, 0:w]),
                 reads=[scr_res[pg]], writes=[slot_res[slot]], dma=True, sem_res=slot_res[slot])
            pstate["fifo"].append((slot, pg))

        def page_acquire(expect):
            slot, pg = pstate["fifo"].pop(0)
            assert pg == expect, (pg, expect)
            return slot

        for s in range(NSLOT):
            page_load(s)

        def rstd_from_ss(ss_ap, ss_res, n, k=1):
            P.op("act", lambda e: e.activation(out=ss_ap, in_=ss_ap, func=AF.Ln, scale=1.0 / n, bias=epsc[:, 0:1]),
                 reads=[ss_res, res("epsc")], writes=[ss_res])
            P.op("act", lambda e: e.activation(out=ss_ap, in_=ss_ap, func=AF.Exp, scale=-0.5),
                 reads=[ss_res], writes=[ss_res])

        def transposes_to(dst_fn, src_ap_fn, src_res, n, banks, dst_res, tag):
            for c in range(n):
                b = banks[c // 4]
                P.op("pe", lambda e, c=c, b=b: e.matmul(pb[b][:, (c % 4) * 128:(c % 4 + 1) * 128],
                                                         lhsT=src_ap_fn(c), rhs=identb[:], start=True, stop=True),
                     reads=[src_res, res("identb")], writes=[BK[b]])
            if n == 8 and banks[1] == banks[0] + 1:
                b0 = banks[0]
                P.op("act", lambda e, b0=b0: e.activation(
                    out=dst_fn(0, 8), in_=pball[:, b0:b0 + 2, :].rearrange("p a (c t) -> p (a c) t", t=128),
                    func=AF.Copy), reads=[BK[b0], BK[b0 + 1]], writes=[dst_res])
                return
            for h in range((n + 3) // 4):
                b = banks[h]
                lo, hi = 4 * h, min(4 * h + 4, n)
                P.op("act", lambda e, b=b, lo=lo, hi=hi: e.activation(
                    out=dst_fn(lo, hi), in_=pb[b][:, 0:(hi - lo) * 128].rearrange("p (c t) -> p c t", t=128),
                    func=AF.Copy), reads=[BK[b]], writes=[dst_res])

        def norm_tile(x_ap, x_res, gain_idx, xn_slot, ss_col, tag):
            ssr = res("ss_col%d" % ss_col)
            ss_ap = stat[:, ss_col:ss_col + 1]
            P.op("act", lambda e: e.activation(out=junk[:], in_=x_ap, func=AF.Square, accum_out=ss_ap),
                 reads=[x_res], writes=[res("PT0"), ssr])
            rstd_from_ss(ss_ap, ssr, D)
            P.op("dve", lambda e: e.scalar_tensor_tensor(out=xn[:, xn_slot, :], in0=x_ap, scalar=ss_ap,
                                                         in1=gains[:, gain_idx, :], op0=ALU.mult, op1=ALU.mult),
                 reads=[x_res, ssr, res("gains")], writes=[res("xn%d" % xn_slot)])

        def rope(dst_ap, src_ap, H, c_ap, s_ap, reads, dst_res, eng2="pool"):
            s3 = src_ap.rearrange("p (h d) -> p h d", d=64)
            t3 = tq[:, 0:H * 64].rearrange("p (h d) -> p h d", d=64)
            w3 = sw[:, 0:H * 64].rearrange("p (h d) -> p h d", d=64)
            d3 = dst_ap.rearrange("p (h d) -> p h d", d=64)
            cb_ = c_ap.unsqueeze(1).to_broadcast([128, H, 64])
            P.op("dve", lambda e: e.tensor_tensor(out=t3, in0=s3, in1=cb_, op=ALU.mult),
                 reads=reads, writes=[res("tq")])
            P.op(eng2, lambda e: e.tensor_tensor(out=w3[:, :, 0:32], in0=s3[:, :, 32:64],
                                                 in1=s_ap[:, 0:32].unsqueeze(1).to_broadcast([128, H, 32]), op=ALU.mult),
                 reads=reads, writes=[res("sw")])
            P.op(eng2, lambda e: e.tensor_tensor(out=w3[:, :, 32:64], in0=s3[:, :, 0:32],
                                                 in1=s_ap[:, 32:64].unsqueeze(1).to_broadcast([128, H, 32]), op=ALU.mult),
                 reads=reads, writes=[res("sw")])
            P.op("dve", lambda e: e.tensor_tensor(out=d3, in0=t3, in1=w3, op=ALU.add),
                 reads=[res("tq"), res("sw")], writes=[dst_res])

        def headnorm(src_ap, src_res, H, ss_col, tag):
            ssr = res("hss")
            ss_ap = stat[:, ss_col:ss_col + H]
            s3 = src_ap.rearrange("p (h d) -> p h d", d=64)
            P.op("act", lambda e: e.activation(out=sw[:, 0:H * 64], in_=src_ap, func=AF.Square),
                 reads=[src_res], writes=[res("sw")])
            P.op("dve", lambda e: e.tensor_reduce(out=ss_ap, in_=sw[:, 0:H * 64].rearrange("p (h d) -> p h d", d=64),
                                                  axis=AX.X, op=ALU.add), reads=[res("sw")], writes=[ssr])
            rstd_from_ss(ss_ap, ssr, 64)
            P.op("dve", lambda e: e.tensor_tensor(out=s3, in0=s3, in1=ss_ap.unsqueeze(2).to_broadcast([128, H, 64]),
                                                  op=ALU.mult), reads=[src_res, ssr], writes=[src_res])

        def fold_tables(slot, rope_ap, gidx):
            tr = res("tabs%d" % slot)
            P.op("pool", lambda e: e.tensor_tensor(out=tabs[:, slot, 0, :], in0=rope_ap[:, 0:64], in1=qkg[:, gidx, :],
                                                   op=ALU.mult), reads=[res("ropet"), res("qkg")], writes=[tr])
            P.op("pool", lambda e: e.tensor_tensor(out=tabs[:, slot, 1, :], in0=rope_ap[:, 64:128],
                                                   in1=qkg[:, gidx + 1, :], op=ALU.mult),
                 reads=[res("ropet"), res("qkg")], writes=[tr])
            return tr

        sgflat = sg[:].rearrange("p a c -> p (a c)")
        P1Q = [(qtok[:, k * 256:(k + 1) * 256], [res("qtok_p1_%d" % k)]) for k in range(4)]
        QS = [(qtok, [res("qtok")] + [res("qtok_p1_%d" % k) for k in range(4)]), (sgflat, [res("sg0"), res("sg1")])]
        NCOL = [0, 3]
        HCOL = [8, 24]
        hsq = mixed[:, 0:512]

        def pipeline(tiles, stages):
            n, ns = len(tiles), len(stages)
            for step in range(n + ns - 1):
                for k in reversed(range(ns)):
                    t = step - k
                    if 0 <= t < n:
                        stages[k](tiles[t])

        def st_load_x(i):
            tt = i % 4
            xr = res("xblk%d" % tt)
            P.op("sp", lambda e: e.dma_start(out=xblk[:, tt, :], in_=x_d[i * 128:(i + 1) * 128, :]),
                 writes=[xr], dma=True, sem_res=xr)

        def st_load_rope(i):
            tt = i % 4
            rr_ = res("ropet%d" % tt)
            P.op("sp", lambda e: e.dma_start(out=ropet[:, tt, :], in_=rope_d[i * 128:(i + 1) * 128, :]),
                 writes=[rr_], dma=True, sem_res=rr_)

        def st_ss(i):
            tt = i % 4
            xr = res("xblk%d" % tt)
            col = NCOL[i % 2]
            ssr = res("ss_col%d" % col)
            ss_ap = stat[:, col:col + 1]
            P.op("act", lambda e: e.activation(out=junk[:], in_=xblk[:, tt, :], func=AF.Square, accum_out=ss_ap),
                 reads=[xr], writes=[res("PT0"), ssr])
            rstd_from_ss(ss_ap, ssr, D)

        def st_xn(i):
            tt = i % 4
            xr = res("xblk%d" % tt)
            col = NCOL[i % 2]
            sl = i % 2
            P.op("dve", lambda e: e.scalar_tensor_tensor(out=xn[:, sl, :], in0=xblk[:, tt, :], scalar=stat[:, col:col + 1],
                                                         in1=gains[:, 0, :], op0=ALU.mult, op1=ALU.mult),
                 reads=[xr, res("ss_col%d" % col), res("gains")], writes=[res("xn%d" % sl)])

        def st_T(i, tb):
            tt = i % 4
            sl = i % 2
            transposes_to(lambda lo, hi: hT[:, lo:hi, tt * 128:(tt + 1) * 128],
                          lambda c: xn[:, sl, c * 128:(c + 1) * 128], res("xn%d" % sl), 8, tb, res("hT%d" % tt), "x")

        def st_xn_T(i, tb):
            tt = i % 4
            xr = res("xblk%d" % tt)
            col = NCOL[i % 2]
            sl = i % 2
            P.op("dve", lambda e: e.scalar_tensor_tensor(out=xn[:, sl, :], in0=xblk[:, tt, :], scalar=stat[:, col:col + 1],
                                                         in1=gains[:, 0, :], op0=ALU.mult, op1=ALU.mult),
                 reads=[xr, res("ss_col%d" % col), res("gains")], writes=[res("xn%d" % sl)])
            transposes_to(lambda lo, hi: hT[:, lo:hi, tt * 128:(tt + 1) * 128],
                          lambda c: xn[:, sl, c * 128:(c + 1) * 128], res("xn%d" % sl), 8, tb, res("hT%d" % tt), "x")

        def st_headnorm(i, H, gidx, part=None, slots=None):
            qs, qres = (slots[i % len(slots)] if slots is not None else QS[i % 2])
            tt = i % 4
            hc = HCOL[i % 2]
            ssr = res("hss%d" % (i % 2))
            ss_ap = stat[:, hc:hc + H]
            src = qs[:, 0:H * 64]
            s3 = src.rearrange("p (h d) -> p h d", d=64)
            if part in (None, "a"):
                P.op("act", lambda e: e.activation(out=hsq[:, 0:H * 64], in_=src, func=AF.Square),
                     reads=qres, writes=[res("mixed")])
                P.op("dve", lambda e: e.tensor_reduce(out=ss_ap, in_=hsq[:, 0:H * 64].rearrange("p (h d) -> p h d", d=64),
                                                      axis=AX.X, op=ALU.add), reads=[res("mixed")], writes=[ssr])
            if part == "a":
                return
            rstd_from_ss(ss_ap, ssr, 64)
            P.op("dve", lambda e: e.tensor_tensor(out=s3, in0=s3, in1=ss_ap.unsqueeze(2).to_broadcast([128, H, 64]),
                                                  op=ALU.mult), reads=qres + [ssr], writes=qres)
            tr = res("tabs%d" % (i % 2))
            rr_ = res("ropet%d" % tt)
            P.op("pool", lambda e: e.tensor_tensor(out=tabs[:, i % 2, 0, :], in0=ropet[:, tt, 0:64], in1=qkg[:, gidx, :],
                                                   op=ALU.mult), reads=[rr_, res("qkg")], writes=[tr])
            P.op("pool", lambda e: e.tensor_tensor(out=tabs[:, i % 2, 1, :], in0=ropet[:, tt, 64:128],
                                                   in1=qkg[:, gidx + 1, :], op=ALU.mult),
                 reads=[rr_, res("qkg")], writes=[tr])

        def st_rope(i, H, slots=None):
            qs, qres = (slots[i % len(slots)] if slots is not None else QS[i % 2])
            tt = i % 4
            sl = i % 2
            qr = res("qrot%d" % sl)
            rope(qrot[:, sl, 0:H * 64], qs[:, 0:H * 64], H, tabs[:, sl, 0, :], tabs[:, sl, 1, :],
                 qres + [res("tabs%d" % sl)], qr)
            rope(qrot[:, sl, H * 64:2 * H * 64], qs[:, H * 64:2 * H * 64], H, ropet[:, tt, 128:192], ropet[:, tt, 192:256],
                 qres + [res("ropet%d" % tt)], qr)

        s_kv0 = page_acquire(PG_KV)
        s_kv1 = page_acquire(PG_KV + 1)

        def p1_proj(i):
            tt = i % 4
            st_load_rope(i)
            kb = 4 + (i % 2)
            qs, qres = P1Q[i % 4]
            for half, sl in ((0, s_kv0), (1, s_kv1)):
                for kc in range(8):
                    P.op("pe", lambda e, kc=kc, half=half, sl=sl: e.matmul(
                        pb[kb][:, half * 256:(half + 1) * 256], lhsT=hT[:, kc, tt * 128:(tt + 1) * 128],
                        rhs=pages[:, sl, kc * 256:(kc + 1) * 256], start=(kc == 0), stop=(kc == 7)),
                        reads=[res("hT%d" % tt), slot_res[sl]], writes=[BK[kb]])
            P.op("act", lambda e: e.activation(out=qs[:, 0:256], in_=pb[kb][:, 0:256], func=AF.Copy),
                 reads=[BK[kb]], writes=qres)
            P.op("dve", lambda e: e.tensor_copy(out=V[:, i, 0, :], in_=pb[kb][:, 256:320]),
                 reads=[BK[kb]], writes=[res("V")])
            P.op("dve", lambda e: e.tensor_copy(out=V[:, i, 5, :], in_=pb[kb][:, 320:384]),
                 reads=[BK[kb]], writes=[res("V")])
            P.op("dve", lambda e: e.tensor_copy(
                out=V[:, i, 2:4, :], in_=pb[kb][:, 384:512].rearrange("p (a d) -> p a d", d=64)),
                reads=[BK[kb]], writes=[res("V")])

        def p1_kT(i):
            sl = i % 2
            kbk = 6 + sl
            for c in range(2):
                P.op("pe", lambda e, c=c: e.matmul(pb[kbk][:, c * 128:(c + 1) * 128],
                                                    lhsT=qrot[:, sl, c * 128:(c + 1) * 128], rhs=identb[:],
                                                    start=True, stop=True),
                     reads=[res("qrot%d" % sl), res("identb")], writes=[BK[kbk]])
            P.op("act", lambda e: e.activation(
                out=kT[:, :, i * 128:(i + 1) * 128], in_=pb[kbk][:, 0:256].rearrange("p (c t) -> p c t", t=128),
                func=AF.Copy), reads=[BK[kbk]], writes=[res("kT")])

        pipeline(list(range(NT)), [
            st_load_x, st_ss, st_xn, lambda i: st_T(i, [(0, 1), (2, 3)][i % 2]), p1_proj,
            lambda i: st_headnorm(i, 2, 2, "a", P1Q), lambda i: st_headnorm(i, 2, 2, "b", P1Q),
            lambda i: st_rope(i, 2, P1Q), p1_kT])
        page_load(s_kv0)
        page_load(s_kv1)

        preloaded = {"x": False}
        blocks_ = list(DBG_BLOCKS(NB))
        for bi_, j in enumerate(blocks_):
            t0 = 4 * j
            sq_slots = [page_acquire(PG_Q + k) for k in range(4)]
            if not preloaded["x"]:
                for tt in range(4):
                    st_load_x(t0 + tt)
                    st_load_rope(t0 + tt)
            preloaded["x"] = False

            def ab_proj(i, sq_slots=sq_slots):
                tt = i % 4
                qb_ = (2, 3) if tt % 2 == 0 else (4, 5)
                qs, qres = QS[i % 2]
                for k in range(4):
                    b = qb_[k // 2]
                    for kc in range(8):
                        P.op("pe", lambda e, k=k, kc=kc, b=b: e.matmul(
                            pb[b][:, (k % 2) * 256:(k % 2 + 1) * 256], lhsT=hT[:, kc, tt * 128:(tt + 1) * 128],
                            rhs=pages[:, sq_slots[k], kc * 256:(kc + 1) * 256], start=(kc == 0), stop=(kc == 7)),
                            reads=[res("hT%d" % tt), slot_res[sq_slots[k]]], writes=[BK[b]])
                P.op("act", lambda e: e.activation(out=qs[:, 0:1024].rearrange("p (a c) -> p a c", a=2),
                                                   in_=pball[:, qb_[0]:qb_[0] + 2, :], func=AF.Copy),
                     reads=[BK[qb_[0]], BK[qb_[1]]], writes=qres)

            def ab_qT(i):
                tt = i % 4
                sl = i % 2
                transposes_to(lambda lo, hi: qT[:, lo:hi, tt * 128:(tt + 1) * 128],
                              lambda c: qrot[:, sl, c * 128:(c + 1) * 128], res("qrot%d" % sl), 8, (6, 7), res("qT"), "q")

            pipeline([t0 + tt for tt in range(4)], [
                st_ss, st_xn, lambda i: st_T(i, (0, 1)), ab_proj,
                lambda i: st_headnorm(i, 8, 0), lambda i: st_rope(i, 8), ab_qT])
            for s_ in sq_slots:
                page_load(s_)

            d_items = [(g, half, nh) for g in range(4) for half in range(2) for nh in range(2)]
            items = []
            for g in range(4):
                for kt in range(NT):
                    items.append(("C", g, kt))
                    cnt_c = len([1 for it in items if it[0] == "C"])
                    if d_items and ((cnt_c % 8 == 0 and cnt_c < 4 * NT) or cnt_c == 4 * NT - 4):
                        items.append(("D",) + d_items.pop(0))
            while d_items:
                items.append(("D",) + d_items.pop(0))
            NI = len(items)
            SBK = ((0, 1), (2, 3))
            pending_stats = []
            cur_iter = {"u": 0}
            PTW4 = PTW
            dcount = {"n": 0}
            dslot = {}

            def xbank(g, half):
                return 4 + (2 * g + half) % 3

            def wvalid(n):
                qi = t0 + n
                return [m for m in (qi - 1, qi, qi + 1) if 0 <= m < NT]

            def flush_stats(b1, now=None, n_max=8):
                colbase = 256
                while pending_stats and n_max > 0 and (now is None or pending_stats[0][0] <= now):
                    pending_stats.pop(0)[1](b1, colbase)
                    colbase += 4
                    n_max -= 1

            def emit_S(u):
                it = items[u]
                b0, b1 = SBK[u % 2]
                if it[0] == "C":
                    _, g, kt = it
                    for half in range(2):
                        b = SBK[u % 2][half]
                        P.op("pe", lambda e, g=g, half=half, kt=kt, b=b: e.matmul(
                            pb[b], lhsT=kT[half * 64:(half + 1) * 64, 0, kt * 128:(kt + 1) * 128],
                            rhs=qT[half * 64:(half + 1) * 64, g, :], start=True, stop=True),
                            reads=[res("kT"), res("qT")], writes=[BK[b]])
                else:
                    _, g, half, nh = it
                    ps2 = pball[:, b0:b0 + 2, :].rearrange("p a c -> p (a c)")
                    for nl in range(2):
                        n = 2 * nh + nl
                        qi = t0 + n
                        for m in wvalid(n):
                            mi = m - (qi - 1)
                            col = nl * 384 + mi * 128
                            P.op("pe", lambda e, m=m, col=col, g=g, half=half, n=n, ps2=ps2, mi=mi: e.matmul(
                                ps2[:, col:col + 128], lhsT=kT[half * 64:(half + 1) * 64, 1, m * 128:(m + 1) * 128],
                                rhs=qT[half * 64:(half + 1) * 64, 4 + g, n * 128:(n + 1) * 128], start=True, stop=(mi == 1)),
                                reads=[res("kT"), res("qT")], writes=[BK[b0 + col // 512]])
                            if mi != 1:
                                P.op("pe", lambda e, col=col, ps2=ps2, mi=mi: e.matmul(
                                    ps2[:, col:col + 128], lhsT=identb[:], rhs=maskb[:, mi * 128:(mi + 1) * 128],
                                    start=False, stop=True),
                                    reads=[res("identb"), res("maskb")], writes=[BK[b0 + col // 512]])

            def emit_E(u):
                it = items[u]
                b0, b1 = SBK[u % 2]
                if it[0] == "C":
                    pt = u % 3
                    P.op("act", lambda e, b0=b0, pt=pt: e.activation(
                        out=PT[:, pt, :].rearrange("p (a c) -> p a c", a=2), in_=pball[:, b0:b0 + 2, :], func=AF.Exp, scale=0.125),
                        reads=[BK[b0], BK[b1]], writes=[res("PT%d" % pt)])
                else:
                    pw = 0
                    dcount["n"] += 1
                    dslot[u] = pw
                    pr = res("PTW%d" % pw)
                    ps2 = pball[:, b0:b0 + 2, :].rearrange("p a c -> p (a c)")
                    P.op("act", lambda e, pw=pw, ps2=ps2: e.activation(out=PTW4[:, pw, :], in_=ps2[:, 0:768], func=AF.Exp, scale=0.125),
                         reads=[BK[b0], BK[b1]], writes=[pr])

            def chunk_finish(c, sl, first):
                qsl = c
                gate = 0
                onr, sqr = res("onrm%d" % sl), res("sqb%d" % qsl)
                P.op("pool", lambda e: e.tensor_tensor(out=sqb[:, qsl, :], in0=onrm[:, sl, :], in1=onrm[:, sl, :],
                                                       op=ALU.mult), reads=[onr], writes=[sqr])
                P.op("dve", lambda e: e.tensor_scalar(out=attnT[:, c, :], in0=onrm[:, sl, :], scalar1=gcol[:, c:c + 1],
                                                      scalar2=None, op0=ALU.mult),
                     reads=[onr, res("gcol")], writes=[res("attnT%d" % c)])
                gbase = 16 + 4 * (c // 4)

                def emit_stats(b1, colbase):
                    for tt in range(4):
                        P.op("pe", lambda e, tt=tt: e.matmul(
                            pb[b1][:, colbase + tt:colbase + tt + 1], lhsT=sqb[:, qsl, tt * 128:(tt + 1) * 128], rhs=ones_c[:, 0:1],
                            start=True, stop=True, skip_group_check=True), reads=[sqr, res("ones_c")], writes=[BK[b1]])
                    if first:
                        P.op("dve", lambda e: e.tensor_copy(out=stat[:, gbase:gbase + 4], in_=pb[b1][:, colbase:colbase + 4]),
                             reads=[BK[b1]], writes=[res("gss")])
                    else:
                        P.op("dve", lambda e: e.tensor_tensor(out=stat[:, gbase:gbase + 4], in0=stat[:, gbase:gbase + 4],
                                                              in1=pb[b1][:, colbase:colbase + 4], op=ALU.add),
                             reads=[BK[b1], res("gss")], writes=[res("gss")])
                pending_stats.append((cur_iter["u"] + gate, emit_stats))

            def epi_half(c, bank, o_lo, den_lo, sink_head, urgent=False):
                rs = c % 2
                sl = c % 2 if c < 4 else 2
                rr, onr = res("rec%d" % rs), res("onrm%d" % sl)
                if sink_head is None and urgent and c == 3:
                    P.op("act", lambda e: e.activation(out=rec[den_lo:den_lo + 64, rs, :], in_=pb[bank][den_lo:den_lo + 64, :],
                                                       func=AF.Ln), reads=[BK[bank]], writes=[rr])
                    P.op("act", lambda e: e.activation(out=rec[den_lo:den_lo + 64, rs, :], in_=rec[den_lo:den_lo + 64, rs, :],
                                                       func=AF.Exp, scale=-1.0), reads=[rr], writes=[rr])
                    P.op("dve", lambda e: e.tensor_copy(out=onrm[o_lo:o_lo + 64, sl, :], in_=rec[den_lo:den_lo + 64, rs, :]),
                         reads=[rr], writes=[onr])
                    P.op("dve", lambda e: e.tensor_tensor(out=onrm[o_lo:o_lo + 64, sl, :], in0=pb[bank][o_lo:o_lo + 64, :],
                                                          in1=onrm[o_lo:o_lo + 64, sl, :], op=ALU.mult),
                         reads=[BK[bank], onr], writes=[onr])
                    return
                if urgent and not _os.environ.get('KNOURGENT'):
                    P.op("act", lambda e: e.activation(out=rec[:, rs, :], in_=pb[bank], func=AF.Copy), reads=[BK[bank]], writes=[rr])
                else:
                    P.op("dve", lambda e: e.tensor_copy(out=rec[:, rs, :], in_=pb[bank]), reads=[BK[bank]], writes=[rr])
                if sink_head is None and urgent and c == 3:
                    P.op("act", lambda e: e.activation(out=rec[den_lo:den_lo + 64, rs, :], in_=rec[den_lo:den_lo + 64, rs, :],
                                                       func=AF.Ln), reads=[rr], writes=[rr])
                    P.op("act", lambda e: e.activation(out=rec[den_lo:den_lo + 64, rs, :], in_=rec[den_lo:den_lo + 64, rs, :],
                                                       func=AF.Exp, scale=-1.0), reads=[rr], writes=[rr])
                    P.op("dve", lambda e: e.tensor_copy(out=onrm[o_lo:o_lo + 64, sl, :], in_=rec[den_lo:den_lo + 64, rs, :]),
                         reads=[rr], writes=[onr])
                elif sink_head is None:
                    P.op("dve", lambda e: e.reciprocal(out=onrm[o_lo:o_lo + 64, sl, :], in_=rec[den_lo:den_lo + 64, rs, :]),
                         reads=[rr], writes=[onr])
                else:
                    P.op("dve", lambda e: e.tensor_scalar(out=onrm[o_lo:o_lo + 64, sl, :], in0=rec[den_lo:den_lo + 64, rs, :],
                                                          scalar1=es[o_lo:o_lo + 64, sink_head:sink_head + 1], scalar2=None,
                                                          op0=ALU.add), reads=[rr, res("es")], writes=[onr])
                    P.op("dve", lambda e: e.reciprocal(out=onrm[o_lo:o_lo + 64, sl, :], in_=onrm[o_lo:o_lo + 64, sl, :]),
                         reads=[onr], writes=[onr])
                P.op("dve", lambda e: e.tensor_tensor(out=onrm[o_lo:o_lo + 64, sl, :], in0=rec[o_lo:o_lo + 64, rs, :],
                                                      in1=onrm[o_lo:o_lo + 64, sl, :], op=ALU.mult), reads=[rr, onr], writes=[onr])

            def emit_PV(u):
                it = items[u]
                if it[0] == "C":
                    _, g, kt = it
                    pt = u % 3
                    ptr = res("PT%d" % pt)
                    for half in range(2):
                        xb_ = xbank(g, half)
                        vb = (0, 2) if half == 0 else (4, 6)
                        P.op("pe", lambda e, kt=kt, pt=pt, xb_=xb_, vb=vb, half=half: e.matmul(
                            pb[xb_], lhsT=V[:, kt, vb[0]:vb[1], :].rearrange("p a d -> p (a d)"),
                            rhs=PT[:, pt, half * 512:(half + 1) * 512],
                            start=(kt == 0), stop=(kt == NT - 1)), reads=[res("V"), res("Vones"), ptr], writes=[BK[xb_]])
                    if kt == NT - 1:
                        epi_half(g, xbank(g, 0), 0, 64, None, urgent=True)
                        epi_half(g, xbank(g, 1), 64, 0, None, urgent=(g == 3))
                        chunk_finish(g, g % 2, g == 0)
                else:
                    _, g, half, nh = it
                    pw = dslot[u]
                    pr = res("PTW%d" % pw)
                    vb = (1, 3) if half == 0 else (3, 5)
                    for nl in range(2):
                        n = 2 * nh + nl
                        qi = t0 + n
                        ms = wvalid(n)
                        for m in ms:
                            col = nl * 384 + (m - (qi - 1)) * 128
                            P.op("pe", lambda e, m=m, col=col, pw=pw, vb=vb, n=n, f=(m == ms[0]), l=(m == ms[-1]): e.matmul(
                                pb[7][:, n * 128:(n + 1) * 128], lhsT=V[:, m, vb[0]:vb[1], :].rearrange("p a d -> p (a d)"),
                                rhs=PTW4[:, pw, col:col + 128], start=f, stop=l, skip_group_check=True),
                                reads=[res("V"), res("Vones"), pr], writes=[BK[7]])
                    if nh == 1:
                        c = 4 + g
                        if half == 0:
                            epi_half(c, 7, 64, 0, g)
                        else:
                            epi_half(c, 7, 0, 64, 4 + g)
                            chunk_finish(c, 2, g == 0)

            emit_S(0)
            if NI > 1:
                emit_S(1)
            late = []
            for u in range(NI):
                cur_iter["u"] = u
                emit_E(u)
                if u + 2 < NI:
                    emit_S(u + 2)
                while late:
                    emit_PV(late.pop(0))
                emit_PV(u)
            while late:
                emit_PV(late.pop(0))
            while pending_stats:
                flush_stats(1, None)

            rstd_from_ss(stat[:, 16:24], res("gss"), 512, 8)
            if dbga_d is not None:
                P.op("pool", lambda e: e.dma_start(out=dbga_d, in_=attnT[:]), reads=[res("attnT%d" % c) for c in range(8)],
                     dma=True, sem_res=res("dbga"))
            so_slots = [page_acquire(PG_O + k) for k in range(4)]
            MS = [(mixed, res("mixed")), (qtok, res("qtok"))]
            MCOL = [1, 4]
            FCOL = [2, 5]

            def e_proj(tt, so_slots=so_slots):
                base = 4 * (tt % 2)
                for grp in range(2):
                    for k in range(4):
                        b = base + 2 * grp + k // 2
                        for c in range(4):
                            cc = 4 * grp + c
                            P.op("pe", lambda e, b=b, k=k, cc=cc, c=c: e.matmul(
                                pb[b][:, (k % 2) * 256:(k % 2 + 1) * 256], lhsT=attnT[:, cc, tt * 128:(tt + 1) * 128],
                                rhs=pages[:, so_slots[k], cc * 256:(cc + 1) * 256], start=(c == 0), stop=(c == 3)),
                                reads=[res("attnT%d" % cc), slot_res[so_slots[k]]], writes=[BK[b]])

            def e_mix(tt):
                base = 4 * (tt % 2)
                mx, mr = MS[tt % 2]
                mx3 = mx[:].rearrange("p (a c) -> p a c", a=2)
                P.op("dve", lambda e: e.tensor_scalar(
                    out=mx3, in0=pball[:, base:base + 2, :], scalar1=stat[:, 16 + tt:17 + tt],
                    scalar2=None, op0=ALU.mult), reads=[BK[base], BK[base + 1], res("gss")], writes=[mr])
                P.op("dve", lambda e: e.scalar_tensor_tensor(
                    out=mx3, in0=pball[:, base + 2:base + 4, :], scalar=stat[:, 20 + tt:21 + tt],
                    in1=mx3, op0=ALU.mult, op1=ALU.add),
                    reads=[BK[base + 2], BK[base + 3], res("gss"), mr], writes=[mr])

            def e_ss1(tt):
                mx, mr = MS[tt % 2]
                col = MCOL[tt % 2]
                ssr = res("ss_col%d" % col)
                P.op("act", lambda e: e.activation(out=junk[:], in_=mx[:], func=AF.Square, accum_out=stat[:, col:col + 1]),
                     reads=[mr], writes=[res("PT0"), ssr])
                rstd_from_ss(stat[:, col:col + 1], ssr, D)

            def e_res(tt):
                mx, mr = MS[tt % 2]
                col = MCOL[tt % 2]
                xr = res("xblk%d" % tt)
                P.op("dve", lambda e: e.scalar_tensor_tensor(out=mx[:], in0=mx[:], scalar=stat[:, col:col + 1],
                                                             in1=gains[:, 1, :], op0=ALU.mult, op1=ALU.mult),
                     reads=[mr, res("ss_col%d" % col), res("gains")], writes=[mr])
                P.op("pool", lambda e: e.tensor_tensor(out=xblk[:, tt, :], in0=xblk[:, tt, :], in1=mx[:],
                                                       op=ALU.add), reads=[xr, mr], writes=[xr])
                if dbg_d is not None:
                    P.op("sp", lambda e, i=t0 + tt: e.dma_start(out=dbg_d[i * 128:(i + 1) * 128, :], in_=xblk[:, tt, :]),
                         reads=[xr], dma=True, sem_res=res("dbg%d" % tt))

            def e_ss2(tt):
                xr = res("xblk%d" % tt)
                col = FCOL[tt % 2]
                ssr = res("ss_col%d" % col)
                P.op("act", lambda e: e.activation(out=junk[:], in_=xblk[:, tt, :], func=AF.Square,
                                                   accum_out=stat[:, col:col + 1]),
                     reads=[xr], writes=[res("PT0"), ssr])
                rstd_from_ss(stat[:, col:col + 1], ssr, D)

            def e_h2(tt):
                xr = res("xblk%d" % tt)
                col = FCOL[tt % 2]
                sl = tt % 2
                P.op("dve", lambda e: e.scalar_tensor_tensor(out=xn[:, sl, :], in0=xblk[:, tt, :], scalar=stat[:, col:col + 1],
                                                             in1=gains[:, 2, :], op0=ALU.mult, op1=ALU.mult),
                     reads=[xr, res("ss_col%d" % col), res("gains")], writes=[res("xn%d" % sl)])

            def e_h2T(tt):
                sl = tt % 2
                tb = [(4, 5), (6, 7)][tt % 2]
                transposes_to(lambda lo, hi: hT[:, lo:hi, tt * 128:(tt + 1) * 128],
                              lambda c: xn[:, sl, c * 128:(c + 1) * 128], res("xn%d" % sl), 8, tb, res("hT%d" % tt), "f")

            pipeline([0, 1, 2, 3], [e_proj, e_mix, e_ss1, e_res, e_ss2, e_h2, e_h2T])
            for s_ in so_slots:
                page_load(s_)

            hT_all = [res("hT%d" % tt) for tt in range(4)]
            for k in range(11):
                sg_ = page_acquire(PG_G + k)
                su_ = page_acquire(PG_U + k)
                for fl in range(2):
                    fc = 2 * k + fl
                    gbk, ubk = (0, 1) if fc % 2 == 0 else (2, 3)
                    for kc in range(8):
                        P.op("pe", lambda e, kc=kc, fl=fl, sg_=sg_, gbk=gbk: e.matmul(
                            pb[gbk], lhsT=pages[:, sg_, kc * 256 + fl * 128:kc * 256 + (fl + 1) * 128],
                            rhs=hT[:, kc, :], start=(kc == 0), stop=(kc == 7)),
                            reads=hT_all + [slot_res[sg_]], writes=[BK[gbk]])
                    for kc in range(8):
                        P.op("pe", lambda e, kc=kc, fl=fl, su_=su_, ubk=ubk: e.matmul(
                            pb[ubk], lhsT=pages[:, su_, kc * 256 + fl * 128:kc * 256 + (fl + 1) * 128],
                            rhs=hT[:, kc, :], start=(kc == 0), stop=(kc == 7)),
                            reads=hT_all + [slot_res[su_]], writes=[BK[ubk]])
                    sgr = res("sg%d" % (fc % 2))
                    P.op("act", lambda e, fc=fc, gbk=gbk: e.activation(out=sg[:, fc % 2, :], in_=pb[gbk], func=AF.Silu),
                         reads=[BK[gbk]], writes=[sgr])
                    P.op("dve", lambda e, fc=fc, ubk=ubk: e.tensor_tensor(out=aT[:, fc, :], in0=sg[:, fc % 2, :],
                                                                          in1=pb[ubk], op=ALU.mult),
                         reads=[sgr, BK[ubk]], writes=[res("aT")])
                page_load(sg_)
                page_load(su_)
            ssf = res("ss_f")
            P.op("act", lambda e: e.activation(out=stat[:, 60:61], in_=epsc[:, 0:1], func=AF.Ln),
                 reads=[res("epsc")], writes=[res("tblwarm")])
            for cb in range(2):
                for grp in range(6):
                    sd_ = page_acquire(PG_D + cb * 6 + grp)
                    nfc = min(4, NFC - 4 * grp)
                    for tt in range(4):
                        bk = 4 * (1 - cb) + tt
                        for f4 in range(nfc):
                            fc = 4 * grp + f4
                            P.op("pe", lambda e, tt=tt, f4=f4, fc=fc, sd_=sd_, bk=bk: e.matmul(
                                pb[bk], lhsT=aT[:, fc, tt * 128:(tt + 1) * 128],
                                rhs=pages[:, sd_, f4 * 512:(f4 + 1) * 512], start=(fc == 0), stop=(fc == NFC - 1)),
                                reads=[res("aT"), slot_res[sd_]], writes=[BK[bk]])
                    page_load(sd_)
                for tt in range(4):
                    bk = 4 * (1 - cb) + tt
                    P.op("act", lambda e, tt=tt, bk=bk, cb=cb: e.activation(
                        out=junk[:, 0:512], in_=pb[bk], func=AF.Square, accum_out=stat[:, 32 + 4 * cb + tt:33 + 4 * cb + tt]),
                        reads=[BK[bk]], writes=[res("PT0"), ssf])
            P.op("dve", lambda e: e.tensor_tensor(out=stat[:, 32:36], in0=stat[:, 32:36], in1=stat[:, 36:40],
                                                  op=ALU.add), reads=[ssf], writes=[ssf])
            rstd_from_ss(stat[:, 32:36], ssf, D, 4)
            for tt in range(4):
                mx, mr = MS[tt % 2]
                for cb in range(2):
                    bk = 4 * (1 - cb) + tt
                    P.op("dve", lambda e, tt=tt, cb=cb, bk=bk, mx=mx: e.scalar_tensor_tensor(
                        out=mx[:, cb * 512:(cb + 1) * 512], in0=pb[bk], scalar=stat[:, 32 + tt:33 + tt],
                        in1=gains[:, 3, cb * 512:(cb + 1) * 512], op0=ALU.mult, op1=ALU.mult),
                        reads=[BK[bk], ssf, res("gains")], writes=[mr])
                xr = res("xblk%d" % tt)
                P.op("pool" if tt % 2 == 0 else "dve", lambda e, tt=tt, mx=mx: e.tensor_tensor(
                    out=xblk[:, tt, :], in0=xblk[:, tt, :], in1=mx[:], op=ALU.add), reads=[xr, mr], writes=[xr])
                i = t0 + tt
                P.op("sp", lambda e, tt=tt, i=i: e.dma_start(out=out_d[i * 128:(i + 1) * 128, :], in_=xblk[:, tt, :]),
                     reads=[xr], dma=True, sem_res=res("ost%d" % tt))
                if bi_ + 1 < len(blocks_) and tt >= 1:
                    st_load_x(4 * blocks_[bi_ + 1] + tt - 1)
                    st_load_rope(4 * blocks_[bi_ + 1] + tt - 1)
            if bi_ + 1 < len(blocks_):
                st_load_x(4 * blocks_[bi_ + 1] + 3)
                st_load_rope(4 * blocks_[bi_ + 1] + 3)
                preloaded["x"] = True
        stats = P.emit()
    return nc, stats


def _rope_table(S):
    t = np.arange(S)
    row = (t // 64).astype(np.float32)
    col = (t % 64).astype(np.float32)
    f_ax = (np.float32(THETA) ** (-(np.arange(16, dtype=np.float32) / np.float32(16)))).astype(np.float32)
    ang_ax = np.concatenate([row[:, None] * f_ax[None, :], col[:, None] * f_ax[None, :]], axis=-1).astype(np.float32)
    f_1d = (np.float32(THETA) ** (-(np.arange(32, dtype=np.float32) / np.float32(32)))).astype(np.float32)
    ang_1d = (t.astype(np.float32)[:, None] * f_1d[None, :]).astype(np.float32)
    tabs = []
    for ang in (ang_ax, ang_1d):
        c = np.cos(ang).astype(np.float32)
        s = np.sin(ang).astype(np.float32)
        tabs.append(np.concatenate([c, c], axis=-1))
        tabs.append(np.concatenate([-s, s], axis=-1))
    return np.ascontiguousarray(np.concatenate(tabs, axis=-1), dtype=np.float32)


def _prep_shared(inp, S):
    w_in = np.asarray(inp["w_in"], np.float32)[0]
    qA, kA, vA = w_in[:, 0:512], w_in[:, 512:640], w_in[:, 640:768]
    qB, kB, vB = w_in[:, 768:1280], w_in[:, 1280:1408], w_in[:, 1408:1536]

    def pair(q):
        return np.concatenate([np.concatenate([q[:, g * 64:(g + 1) * 64], q[:, (4 + g) * 64:(5 + g) * 64]], 1)
                               for g in range(4)], 1)

    w_kv = np.concatenate([kA, kB, vA, vB], 1)
    w_q = np.concatenate([pair(qA), pair(qB)], 1)
    w_out = np.asarray(inp["w_out"], np.float32)[0]
    rows = []
    for c in range(4):
        rows += list(range(c * 64, (c + 1) * 64)) + list(range((4 + c) * 64, (5 + c) * 64))
    for c in range(4):
        rows += list(range(512 + (4 + c) * 64, 512 + (5 + c) * 64)) + list(range(512 + c * 64, 512 + (c + 1) * 64))
    w_o = w_out[np.array(rows)]
    ga = np.asarray(inp["group_norm_a"], np.float32)[0]
    gb = np.asarray(inp["group_norm_b"], np.float32)[0]
    gcat = np.concatenate([ga, gb])
    gcol = gcat[np.array(rows)].reshape(8, 128).T
    qn = np.asarray(inp["q_norm_a"], np.float32)[0]
    kn = np.asarray(inp["k_norm_a"], np.float32)[0]
    sw_ = lambda g: np.concatenate([g[32:], g[:32]])
    qk = np.stack([qn, sw_(qn), kn, sw_(kn)], 0)
    gains = np.stack([np.asarray(inp[k], np.float32)[0] for k in
                      ("norm_mix_pre", "norm_mix_post", "norm_ffn_pre", "norm_ffn_post")], 0)
    kk = np.arange(128)[:, None]
    qq = np.arange(128)[None, :]
    valid = np.concatenate([(qq <= kk), np.ones((128, 128), bool), (kk <= qq)], 1)
    mask = np.where(valid, 0.0, -30000.0).astype(np.float32)
    c = np.ascontiguousarray
    return {
        "w_kv": c(w_kv), "w_q": c(w_q), "w_o": c(w_o),
        "w_g": c(np.asarray(inp["w_gate"], np.float32)[0]), "w_u": c(np.asarray(inp["w_up"], np.float32)[0]),
        "w_d": c(np.asarray(inp["w_down"], np.float32)[0]),
        "gains": c(gains), "qk_gain": c(qk), "sink": c(np.asarray(inp["sink_b"], np.float32).reshape(1, 8)),
        "gcol": c(gcol.astype(np.float32)), "rope": _rope_table(S), "ident": np.eye(128, dtype=np.float32),
        "mask": c(mask),
    }


_CACHE = {}


def kernel(**inputs):
    x = np.asarray(inputs["x"], np.float32)
    B, S, _ = x.shape
    shared = _prep_shared(inputs, S)
    if S not in _CACHE:
        _CACHE[S] = build_program(S)[0]
    nc = _CACHE[S]
    in_maps = []
    for b in range(B):
        m = dict(shared)
        m["x"] = np.ascontiguousarray(x[b])
        in_maps.append(m)
    res = run_bass_kernel_spmd(nc, in_maps, core_ids=list(range(B)))
    return np.stack([np.asarray(r["out"], np.float32) for r in res.results], 0)
```
